# Optimizing a Trainium2 kernel written in Bass

```python
import math
import jax, jax.numpy as jnp
from jax import lax
import numpy as np

D_MODEL = 1024
BATCH = 8
SEQ = 8192
DEPTH = 4
DEC_BATCH = 32
DEC_SEQ = 16
PAST_LEN = 1024

CHUNK = 64
N_META = 16
N_EVEN = (DEPTH + 1) // 2
N_ODD = DEPTH // 2
D_A = D_MODEL // 2
D_B = D_MODEL // 2
W_A = 31
W_B = 3
W_C = 4
D_RNN = D_MODEL
LRU_HEADS = 8
LRU_BW = D_RNN // LRU_HEADS
LRU_C = 8.0
D_FF = 4 * D_MODEL
P_EVEN = 2 * D_A + 3 * D_B
P_ODD = 2 * D_RNN
ALPHA = (2 * DEPTH) ** 0.25
BETA = (8 * DEPTH) ** -0.25
LN_EPS = 1e-5

kernel_name = 'hybrid_streaming_conv_rglru_encoder_step'


def layer_norm(x, g, b):
    xf = x.astype(jnp.float32)
    mu = jnp.mean(xf, axis=-1, keepdims=True)
    var = jnp.mean(jnp.square(xf - mu), axis=-1, keepdims=True)
    y = (xf - mu) * lax.rsqrt(var + LN_EPS)
    return (y * g.astype(jnp.float32) + b.astype(jnp.float32)).astype(x.dtype)


def causal_dwconv(x, buf, w):
    xp = jnp.concatenate([buf.astype(x.dtype), x], axis=1)
    out = lax.conv_general_dilated(
        xp, w[:, None, :].astype(x.dtype), window_strides=(1,), padding='VALID',
        dimension_numbers=('NWC', 'WIO', 'NWC'), feature_group_count=x.shape[-1])
    new_buf = xp[:, xp.shape[1] - (w.shape[0] - 1):]
    return out, new_buf


def linear_recurrence(a, b, h0):
    b = b.at[:, 0].add(a[:, 0] * h0)
    def combine(l, r):
        return (l[0] * r[0], r[0] * l[1] + r[1])
    _, h = lax.associative_scan(combine, (a, b), axis=1)
    return h


def conv_pair_mixer(x, buf_a, buf_b, w_in, b_in, conv_a_w, conv_a_b, ln_a_g, ln_a_b, conv_b_w, w_out, b_out):
    p = jnp.einsum('btd,dp->btp', x, w_in) + b_in
    a_val = p[..., :D_A]
    a_gate = p[..., D_A:2 * D_A]
    g_b = p[..., 2 * D_A:2 * D_A + D_B]
    g_c = p[..., 2 * D_A + D_B:2 * D_A + 2 * D_B]
    h_b = p[..., 2 * D_A + 2 * D_B:]
    u = a_val * jax.nn.sigmoid(a_gate)
    ca, new_buf_a = causal_dwconv(u, buf_a, conv_a_w)
    y_a = jax.nn.silu(layer_norm(ca + conv_a_b, ln_a_g, ln_a_b))
    v = g_c * h_b
    cb, new_buf_b = causal_dwconv(v, buf_b, conv_b_w)
    y_b = g_b * cb
    y = jnp.einsum('btc,cd->btd', jnp.concatenate([y_a, y_b], axis=-1), w_out) + b_out
    return y, new_buf_a, new_buf_b


def rglru_mixer(x, buf_c, h0, w_in, b_in, conv_c_w, conv_c_b, w_gate_a, b_gate_a, w_gate_x, b_gate_x, lru_lambda, w_out, b_out):
    bsz, t = x.shape[0], x.shape[1]
    p = jnp.einsum('btd,dp->btp', x, w_in) + b_in
    gate = p[..., :D_RNN]
    u = p[..., D_RNN:]
    xc, new_buf_c = causal_dwconv(u, buf_c, conv_c_w)
    xc = xc + conv_c_b
    xh = xc.reshape(bsz, t, LRU_HEADS, LRU_BW)
    r = jax.nn.sigmoid(jnp.einsum('bthi,hij->bthj', xh, w_gate_a).reshape(bsz, t, D_RNN) + b_gate_a)
    i = jax.nn.sigmoid(jnp.einsum('bthi,hij->bthj', xh, w_gate_x).reshape(bsz, t, D_RNN) + b_gate_x)
    log_a = -LRU_C * r.astype(jnp.float32) * jax.nn.softplus(-lru_lambda.astype(jnp.float32))
    a = jnp.exp(log_a)
    mult = jnp.sqrt(jnp.maximum(-jnp.expm1(2.0 * log_a), 0.0))
    bterm = mult * (i.astype(jnp.float32) * xc.astype(jnp.float32))
    h = linear_recurrence(a, bterm, h0.astype(jnp.float32))
    y = h.astype(x.dtype) * jax.nn.gelu(gate)
    out = jnp.einsum('btr,rd->btd', y, w_out) + b_out
    return out, new_buf_c, h[:, -1]


def squared_relu_mlp(x, w1, w2):
    hid = jnp.square(jax.nn.relu(jnp.einsum('btd,df->btf', x, w1)))
    return jnp.einsum('btf,fd->btd', hid, w2)


def trunk(x, buf_a, buf_b, buf_c, h_lru, prm):
    new_a, new_b, new_c, new_h = [], [], [], []
    for l in range(DEPTH):
        if l % 2 == 0:
            e = l // 2
            m, na, nb = conv_pair_mixer(
                x, buf_a[e], buf_b[e], prm['w_in_e'][e], prm['b_in_e'][e], prm['conv_a_w'][e], prm['conv_a_b'][e],
                prm['ln_a_g'][e], prm['ln_a_b'][e], prm['conv_b_w'][e], prm['w_out_e'][e], prm['b_out_e'][e])
            new_a.append(na)
            new_b.append(nb)
        else:
            o = l // 2
            m, nc, nh = rglru_mixer(
                x, buf_c[o], h_lru[o], prm['w_in_o'][o], prm['b_in_o'][o], prm['conv_c_w'][o], prm['conv_c_b'][o],
                prm['w_gate_a'][o], prm['b_gate_a'][o], prm['w_gate_x'][o], prm['b_gate_x'][o], prm['lru_lambda'][o],
                prm['w_out_o'][o], prm['b_out_o'][o])
            new_c.append(nc)
            new_h.append(nh)
        x = layer_norm(ALPHA * x + m, prm['ln1_g'][l], prm['ln1_b'][l])
        x = layer_norm(ALPHA * x + squared_relu_mlp(x, prm['w_mlp1'][l], prm['w_mlp2'][l]), prm['ln2_g'][l], prm['ln2_b'][l])
    return x, jnp.stack(new_a), jnp.stack(new_b), jnp.stack(new_c), jnp.stack(new_h)


def setup_inputs(seed: int = 0) -> dict:
    key = jax.random.key(seed)
    ks = iter(jax.random.split(key, 40))
    f32 = jnp.float32
    def nrm(shape, scale):
        return jax.random.normal(next(ks), shape, f32) * scale
    u = jax.random.uniform(next(ks), (N_ODD, D_RNN), f32, 0.9, 0.999)
    s = u ** (1.0 / LRU_C)
    lru_lambda = jnp.log(s) - jnp.log1p(-s)
    return {
        'x_prompt': nrm((BATCH, SEQ, D_MODEL), 1.0),
        'x_sample': nrm((DEC_BATCH, DEC_SEQ, D_MODEL), 1.0),
        'state_conv_a': nrm((N_EVEN, DEC_BATCH, W_A - 1, D_A), 1.0),
        'state_conv_b': nrm((N_EVEN, DEC_BATCH, W_B - 1, D_B), 1.0),
        'state_conv_c': nrm((N_ODD, DEC_BATCH, W_C - 1, D_RNN), 1.0),
        'state_lru': nrm((N_ODD, DEC_BATCH, D_RNN), 0.5),
        'meta_tokens': nrm((N_META, D_MODEL), 1.0),
        'ln1_g': 1.0 + nrm((DEPTH, D_MODEL), 0.02),
        'ln1_b': nrm((DEPTH, D_MODEL), 0.02),
        'ln2_g': 1.0 + nrm((DEPTH, D_MODEL), 0.02),
        'ln2_b': nrm((DEPTH, D_MODEL), 0.02),
        'w_in_e': nrm((N_EVEN, D_MODEL, P_EVEN), D_MODEL ** -0.5),
        'b_in_e': nrm((N_EVEN, P_EVEN), 0.02),
        'conv_a_w': nrm((N_EVEN, W_A, D_A), W_A ** -0.5),
        'conv_a_b': nrm((N_EVEN, D_A), 0.02),
        'ln_a_g': 1.0 + nrm((N_EVEN, D_A), 0.02),
        'ln_a_b': nrm((N_EVEN, D_A), 0.02),
        'conv_b_w': nrm((N_EVEN, W_B, D_B), W_B ** -0.5),
        'w_out_e': nrm((N_EVEN, D_A + D_B, D_MODEL), BETA * (D_A + D_B) ** -0.5),
        'b_out_e': nrm((N_EVEN, D_MODEL), 0.02),
        'w_in_o': nrm((N_ODD, D_MODEL, P_ODD), D_MODEL ** -0.5),
        'b_in_o': nrm((N_ODD, P_ODD), 0.02),
        'conv_c_w': nrm((N_ODD, W_C, D_RNN), W_C ** -0.5),
        'conv_c_b': nrm((N_ODD, D_RNN), 0.02),
        'w_gate_a': nrm((N_ODD, LRU_HEADS, LRU_BW, LRU_BW), LRU_BW ** -0.5),
        'b_gate_a': nrm((N_ODD, D_RNN), 0.02),
        'w_gate_x': nrm((N_ODD, LRU_HEADS, LRU_BW, LRU_BW), LRU_BW ** -0.5),
        'b_gate_x': nrm((N_ODD, D_RNN), 0.02),
        'lru_lambda': lru_lambda,
        'w_out_o': nrm((N_ODD, D_RNN, D_MODEL), BETA * D_RNN ** -0.5),
        'b_out_o': nrm((N_ODD, D_MODEL), 0.02),
        'w_mlp1': nrm((DEPTH, D_MODEL, D_FF), D_MODEL ** -0.5),
        'w_mlp2': nrm((DEPTH, D_FF, D_MODEL), BETA * D_FF ** -0.5),
    }


def reference(x_prompt, x_sample, state_conv_a, state_conv_b, state_conv_c, state_lru, meta_tokens,
              ln1_g, ln1_b, ln2_g, ln2_b,
              w_in_e, b_in_e, conv_a_w, conv_a_b, ln_a_g, ln_a_b, conv_b_w, w_out_e, b_out_e,
              w_in_o, b_in_o, conv_c_w, conv_c_b, w_gate_a, b_gate_a, w_gate_x, b_gate_x, lru_lambda, w_out_o, b_out_o,
              w_mlp1, w_mlp2):
    prm = dict(ln1_g=ln1_g, ln1_b=ln1_b, ln2_g=ln2_g, ln2_b=ln2_b,
               w_in_e=w_in_e, b_in_e=b_in_e, conv_a_w=conv_a_w, conv_a_b=conv_a_b, ln_a_g=ln_a_g, ln_a_b=ln_a_b,
               conv_b_w=conv_b_w, w_out_e=w_out_e, b_out_e=b_out_e,
               w_in_o=w_in_o, b_in_o=b_in_o, conv_c_w=conv_c_w, conv_c_b=conv_c_b, w_gate_a=w_gate_a,
               b_gate_a=b_gate_a, w_gate_x=w_gate_x, b_gate_x=b_gate_x, lru_lambda=lru_lambda,
               w_out_o=w_out_o, b_out_o=b_out_o, w_mlp1=w_mlp1, w_mlp2=w_mlp2)
    bsz = x_prompt.shape[0]
    dt = x_prompt.dtype
    meta = jnp.broadcast_to(meta_tokens.astype(dt)[None], (bsz, N_META, D_MODEL))
    xp = jnp.concatenate([meta, x_prompt], axis=1)
    zero_a = jnp.zeros((N_EVEN, bsz, W_A - 1, D_A), dt)
    zero_b = jnp.zeros((N_EVEN, bsz, W_B - 1, D_B), dt)
    zero_c = jnp.zeros((N_ODD, bsz, W_C - 1, D_RNN), dt)
    zero_h = jnp.zeros((N_ODD, bsz, D_RNN), jnp.float32)
    yp_full, sa_p, sb_p, sc_p, sh_p = trunk(xp, zero_a, zero_b, zero_c, zero_h, prm)
    y_prompt = yp_full[:, N_META:]
    y_sample, sa_s, sb_s, sc_s, sh_s = trunk(x_sample, state_conv_a, state_conv_b, state_conv_c, state_lru, prm)
    return (y_prompt, y_sample, sa_p, sb_p, sc_p, sh_p, sa_s, sb_s, sc_s, sh_s)
```

```python
import numpy as np
import concourse.bass as bass
import concourse.mybir as mybir
from concourse.bass_utils import run_bass_kernel_spmd

F32 = mybir.dt.float32
F32R = mybir.dt.float32r
AF = mybir.ActivationFunctionType
ALU = mybir.AluOpType

D = 1024
DEPTH = 4
SEQ = 8192
NMETA = 16
NPROMPT = SEQ + NMETA
NSAMP = 4
TS = 16
NTOK = NPROMPT + NSAMP * TS
CH = 486
NFULL = 16
LASTP = NPROMPT - NFULL * CH
ALPHA = (2 * DEPTH) ** 0.25
LN_EPS = 1e-5
TP = 512
R_SLOTS = 3
DEFER_IN = True
K0A = 9
NDA = (4 * (31 - K0A) + 31) // 32
NSEQ = 5

EVEN_IN_ORDER = [1, 0, 4, 3, 2]
ODD_IN_ORDER = [2, 3, 0, 1]


def tile_plan():
    plan = []
    for l in range(DEPTH):
        if l % 2 == 0:
            plan.append(("w_in_e", l // 2, 1))
            plan.append(("w_in_e", l // 2, 0))
            for t in range(NDA):
                plan.append(("diag_a", l // 2, t))
            plan.append(("w_in_e", l // 2, 4))
            plan.append(("w_in_e", l // 2, 3))
            plan.append(("w_in_e", l // 2, 2))
        else:
            plan.append(("w_in_o", l // 2, 2))
            plan.append(("w_in_o", l // 2, 3))
            plan.append(("diag_c", l // 2, 0))
            plan.append(("w_in_o", l // 2, 0))
            plan.append(("w_in_o", l // 2, 1))
        for g in range(2):
            plan.append(("w_out_e" if l % 2 == 0 else "w_out_o", l // 2, g))
        for g in range(8):
            plan.append(("w_mlp1", l, g))
        for g in range(8):
            plan.append(("w_mlp2", l, g))
    return plan


PLAN = tile_plan()
NTILES = len(PLAN)

PCOL = {}
_pc = 0


def _padd(name, n):
    global _pc
    PCOL[name] = _pc
    _pc += n


for _l in range(DEPTH):
    for _nm in ("ln1_g", "ln1_b", "ln2_g", "ln2_b"):
        _padd((_nm, _l), 8)
for _e in range(2):
    _padd(("b_in_e", _e), 20)
    _padd(("conv_a_w", _e), 124)
    _padd(("conv_a_b", _e), 4)
    _padd(("ln_a_g", _e), 4)
    _padd(("ln_a_b", _e), 4)
    _padd(("conv_b_w", _e), 12)
    _padd(("b_out_e", _e), 8)
for _o in range(2):
    _padd(("b_in_o", _o), 16)
    _padd(("conv_c_w", _o), 32)
    _padd(("conv_c_b", _o), 8)
    _padd(("b_gate_a", _o), 8)
    _padd(("b_gate_x", _o), 8)
    _padd(("lru_lambda", _o), 8)
    _padd(("b_out_o", _o), 8)
NPRM = _pc

SA_OFF = 0
SB_OFF = SA_OFF + 2 * 4 * NSEQ * 30
SC_OFF = SB_OFF + 2 * 4 * NSEQ * 2
SH_OFF = SC_OFF + 2 * 8 * NSEQ * 3
NST = SH_OFF + 2 * 8 * NSEQ

_off = 0


def _alloc(n):
    global _off
    o = _off
    _off += n
    return o


X_OFF = _alloc(8 * TP)
X1_OFF = _alloc(8 * TP)
AR_OFF = _alloc(20480)
YO_OFF = AR_OFF + 16384
SQ_OFF = _alloc(4 * TP)
MSQ_OFF = _alloc(TP)
RSTD_OFF = _alloc(TP)
TN_OFF = _alloc(3 * TP)
W_OFF = _alloc(R_SLOTS * 4096)
GW_OFF = _alloc(4096)
PRM_OFF = _alloc(NPRM)
ST_OFF = _alloc(NST)
ONES_OFF = _alloc(256)
DER_OFF = _alloc(96)
CST_OFF = _alloc(4)
BS_OFF = MSQ_OFF + 496


def PB_LOC(i):
    return (SQ_OFF + i * TP + 496) if i < 4 else (TN_OFF + (i - 4) * TP + 496)

S_TOTAL = _off

WU = 30 + LASTP + NSAMP * (30 + TS)
WV = 2 + LASTP + NSAMP * (2 + TS)
WC = 3 + LASTP + NSAMP * (3 + TS)
E_UBUF = AR_OFF
E_VBUF = E_UBUF + 4 * WU
E_GB = E_VBUF + 4 * WV
E_SG = E_GB + 4 * TP
E_HB = E_SG + 4 * TP
E_ACC = E_HB + 4 * TP
E_CA = E_ACC + 4 * TP
E_Y = E_CA + 4 * TP
E_ZT = E_Y + 8 * TP
assert E_ZT + 2 * TP <= AR_OFF + 20480
O_CBUF = AR_OFF
O_XC = AR_OFF + 4096
O_RA = AR_OFF + 8192
O_IB = AR_OFF + 12288
O_A2M = AR_OFF + 16384
M_HID = AR_OFF
M_ZT = AR_OFF + 16384

PS_MAIN = [0, 1, 2, 3, 6, 7]
PS_MEAN = 4
PS_EZ2 = 5


class View:
    __slots__ = ("ap", "keys")

    def __init__(self, ap, keys):
        self.ap = ap
        self.keys = keys


GRAN = 128


def _sb_keys(lo, hi):
    return list(range(lo // GRAN, (hi - 1) // GRAN + 1))


class Prog:
    def __init__(self, nc, S, psum, sems):
        self.nc = nc
        self.S = S
        self.psum = psum
        self.sems = sems
        self.count = {k: 0 for k in sems}
        self.lists = {"pe": [], "act": [], "dve": [], "pool": [], "sp": []}
        self.lastw = {}
        self.readers = {}
        self.waited = {k: {} for k in self.lists}
        self.mult = {k: 1 for k in sems}
        self.targets = {}

    def sb(self, off, n, r=False):
        return View(None, _sb_keys(off, off + n))

    def sb3(self, off, nch, width, c0, c1, r=False, j0=0, j1=None):
        if j1 is None:
            j1 = nch
        return View(None, _sb_keys(off + j0 * width, off + j1 * width))

    def ps(self, bank, c0=0, c1=TP):
        return View(self.psum[bank][:, c0:c1], [100000 + bank])

    def _deps(self, reads, writes):
        deps = {}

        def add(k, c):
            if deps.get(k, 0) < c:
                deps[k] = c

        for v in reads:
            for g in v.keys:
                w = self.lastw.get(g)
                if w:
                    add(*w)
        for v in writes:
            for g in v.keys:
                w = self.lastw.get(g)
                if w:
                    add(*w)
                rd = self.readers.get(g)
                if rd:
                    for k, c in rd.items():
                        add(k, c)
        return deps

    def _emit_waits(self, ek, deps, selfkey):
        lst = self.lists[ek]
        for k, c in deps.items():
            if k == selfkey and ek == "pe":
                continue
            if self.waited[ek].get(k, 0) >= c:
                continue
            self.waited[ek][k] = c
            self.targets.setdefault(k, set()).add(c)
            lst.append(("wait", k, c * self.mult[k]))

    def _record(self, key, cnt, reads, writes):
        for v in writes:
            for g in v.keys:
                self.lastw[g] = (key, cnt)
                self.readers[g] = {}
        for v in reads:
            for g in v.keys:
                d = self.readers.setdefault(g, {})
                if d.get(key, 0) < cnt:
                    d[key] = cnt

    def op(self, ek, fn, reads=(), writes=(), signal=True):
        deps = self._deps(reads, writes)
        self._emit_waits(ek, deps, ek)
        if signal:
            self.count[ek] += 1
            cnt = self.count[ek]
            self.lists[ek].append(("op", fn, ek, 1))
        else:
            cnt = self.count[ek] + 1
            self.lists[ek].append(("op", fn, None, 0))
        self._record(ek, cnt, reads, writes)

    def dma(self, qe, semname, fn, reads=(), writes=()):
        deps = self._deps(reads, writes)
        self._emit_waits(qe, deps, None)
        self.count[semname] += 1
        cnt = self.count[semname]
        self.lists[qe].append(("op", fn, semname, 16))
        self._record(semname, cnt, reads, writes)

    def wait_all(self, ek, semname):
        self.lists[ek].append(("wait", semname, self.count[semname] * self.mult[semname]))

    def replay(self, ek, eng):
        pending = {}
        done = {}
        for it in self.lists[ek]:
            if it[0] == "wait":
                eng.wait_ge(self.sems[it[1]], it[2])
                continue
            ins = it[1](eng)
            k = it[2]
            if k is None:
                continue
            if it[3] != 1:
                ins.then_inc(self.sems[k], it[3])
                continue
            done[k] = done.get(k, 0) + 1
            pending[k] = pending.get(k, 0) + 1
            if done[k] in self.targets.get(k, ()) or pending[k] >= 15:
                ins.then_inc(self.sems[k], pending[k])
                pending[k] = 0


def chunk_groups(c):
    if c < NFULL:
        return [([0], 0, CH, 1)], CH
    return [([0], 0, LASTP, 1), ([1, 2, 3, 4], LASTP, TS, NSAMP)], LASTP + NSAMP * TS


def build_program():
    nc = bass.Bass("TRN2", target_bir_lowering=False)
    xin = nc.dram_tensor("xin", [D, NTOK], F32, kind="ExternalInput").ap()
    wts = nc.dram_tensor("wts", [NTILES, 128, 4096], F32, kind="ExternalInput").ap()
    gws = nc.dram_tensor("gws", [128, 4096], F32, kind="ExternalInput").ap()
    prm = nc.dram_tensor("prm", [128, NPRM], F32, kind="ExternalInput").ap()
    sti = nc.dram_tensor("sti", [128, NST], F32, kind="ExternalInput").ap()
    yout = nc.dram_tensor("yout", [D, NTOK], F32, kind="ExternalOutput").ap()
    sto = nc.dram_tensor("sto", [128, NST], F32, kind="ExternalOutput").ap()

    from contextlib import ExitStack
    with ExitStack() as es:
        slab = es.enter_context(nc.sbuf_tensor("slab", [128, S_TOTAL], F32))
        sbase = nc.lookup_mloc(slab).addr

        class _SProxy:
            WSTRIDE = 2048
            WSIZE = 8192

            def __init__(self):
                self.win = {}

            def __getitem__(self, key):
                _, sl = key
                a, b = sl.start, sl.stop
                st = (a // self.WSTRIDE) * self.WSTRIDE
                size = min(self.WSIZE, S_TOTAL - st)
                assert b <= st + size, (a, b)
                h = self.win.get(st)
                if h is None:
                    h = nc.alloc_sbuf_tensor_at(f"F{st}", [128, size], F32, offset=sbase + st * 4)
                    self.win[st] = h
                return h[:, a - st:b - st]

        S = _SProxy()
        RT = {}

        def addR(name, off, n):
            RT[name] = (off, n, nc.alloc_sbuf_tensor_at(name, [128, n], F32R, offset=sbase + off * 4))

        addR("X", X_OFF, 8 * TP)
        addR("X1", X1_OFF, 8 * TP)
        addR("W", W_OFF, R_SLOTS * 4096)
        addR("GW", GW_OFF, 4096)
        addR("ONES", ONES_OFF, 256)
        addR("SQ", SQ_OFF, 4 * TP)
        addR("HID", M_HID, 32 * TP)
        addR("ECA", E_CA, 4 * TP)
        addR("EY", E_Y, 8 * TP)
        addR("OXC", O_XC, 8 * TP)
        addR("OYR", O_A2M, 8 * TP)
        addR("EUB", E_UBUF, 4 * WU)
        addR("OCB", O_CBUF, 8 * WC)

        def rap(name, a, b):
            off, n, h = RT[name]
            assert off <= a and b <= off + n, (name, a, b)
            return h[:, a - off:b - off]

        GT_MLP = M_HID + 8 * TP
        RNAME = {GT_MLP: "HID", X_OFF: "X", X1_OFF: "X1", E_Y: "EY", O_A2M: "OYR", E_CA: "ECA", O_XC: "OXC", M_HID: "HID", E_UBUF: "EUB", O_CBUF: "OCB"}
        psum = [es.enter_context(nc.psum_tensor(f"ps{i}", [128, TP], F32)) for i in range(8)]
        semnames = ["pe", "act", "dve", "pool", "x", "x1", "y", "p0", "p1", "p2", "so"] + [f"w{i}" for i in range(R_SLOTS)]
        sems = {n: es.enter_context(nc.semaphore(n)) for n in semnames}
        P = Prog(nc, S, psum, sems)
        for n in ["x", "x1", "y", "p0", "p1", "p2", "so"] + [f"w{i}" for i in range(R_SLOTS)]:
            P.mult[n] = 16
        block = es.enter_context(nc.Block())

        def pcol(name, j=0):
            o = PRM_OFF + PCOL[name] + j
            return S[:, o:o + 1]

        prm_v = P.sb(PRM_OFF, NPRM)
        st_v = P.sb(ST_OFF, NST)
        der_v = P.sb(DER_OFF, 96)
        cst_v = P.sb(CST_OFF, 4)
        ones_v = P.sb(ONES_OFF, 256, r=True)
        gw_v = P.sb(GW_OFF, 4096, r=True)
        ONE_AP = S[:, CST_OFF:CST_OFF + 1]
        EPS_AP = S[:, CST_OFF + 1:CST_OFF + 2]
        ZERO_AP = S[:, CST_OFF + 2:CST_OFF + 3]

        P.dma("sp", "p0", lambda e: e.dma_start(out=S[:, PRM_OFF:PRM_OFF + NPRM], in_=prm), writes=[prm_v])
        P.dma("sp", "p1", lambda e: e.dma_start(out=S[:, ST_OFF:ST_OFF + NST], in_=sti), writes=[st_v])
        P.dma("pool", "p2", lambda e: e.dma_start(out=rap("GW", GW_OFF, GW_OFF + 4096), in_=gws), writes=[gw_v])
        bs_v = P.sb(BS_OFF, 8)
        for i_ in range(7):
            nm_ = ("ln1_b", i_) if i_ < 4 else ("ln2_b", i_ - 4)
            pl_ = PB_LOC(i_)
            P.op("dve", lambda e, pl_=pl_: e.memset(S[:, pl_:pl_ + 16], 0.0), writes=[P.sb(pl_, 16)])
            src_ = S[:, PRM_OFF + PCOL[nm_]:PRM_OFF + PCOL[nm_] + 8].rearrange("p (j c) -> p j c", c=1)
            dst_ = S[:, pl_:pl_ + 16].rearrange("p (j c) -> p j c", c=2)[:, :, 0:1]
            P.op("dve", lambda e, src_=src_, dst_=dst_: e.tensor_copy(out=dst_, in_=src_), reads=[prm_v], writes=[P.sb(pl_, 16)])
        tn0_v = P.sb(TN_OFF, 256)
        P.op("dve", lambda e: e.memset(S[:, TN_OFF:TN_OFF + 128], 1.0 / 1024.0), writes=[tn0_v])
        P.op("dve", lambda e: e.memset(S[:, TN_OFF + 128:TN_OFF + 256], 1.0 / 512.0), writes=[tn0_v])
        P.op("dve", lambda e: e.tensor_copy(out=rap("ONES", ONES_OFF, ONES_OFF + 256), in_=S[:, TN_OFF:TN_OFF + 256]),
             reads=[tn0_v], writes=[ones_v])
        P.op("dve", lambda e: e.memset(S[:, CST_OFF:CST_OFF + 1], 1.0), writes=[cst_v])
        P.op("dve", lambda e: e.memset(S[:, CST_OFF + 1:CST_OFF + 2], LN_EPS), writes=[cst_v])
        P.op("dve", lambda e: e.memset(S[:, CST_OFF + 2:CST_OFF + 4], 0.0), writes=[cst_v])
        P.op("dve", lambda e: e.memset(S[:, DER_OFF + 80:DER_OFF + 81], 0.25), writes=[der_v])
        for o in range(2):
            lam = S[:, PRM_OFF + PCOL[("lru_lambda", o)]:PRM_OFF + PCOL[("lru_lambda", o)] + 8]
            tmp = S[:, DER_OFF + 32 + o * 8:DER_OFF + 40 + o * 8]
            cc = S[:, DER_OFF + o * 8:DER_OFF + o * 8 + 8]
            c2 = S[:, DER_OFF + 16 + o * 8:DER_OFF + 24 + o * 8]
            P.op("act", lambda e, lam=lam, tmp=tmp: e.activation(out=tmp, in_=lam, func=AF.Exp, scale=-1.0),
                 reads=[prm_v], writes=[der_v])
            P.op("act", lambda e, tmp=tmp: e.activation(out=tmp, in_=tmp, func=AF.Ln, bias=ONE_AP, scale=1.0),
                 reads=[der_v, cst_v], writes=[der_v])
            P.op("dve", lambda e, tmp=tmp, cc=cc: e.tensor_scalar(out=cc, in0=tmp, scalar1=-8.0, scalar2=None, op0=ALU.mult),
                 reads=[der_v], writes=[der_v])
            P.op("dve", lambda e, tmp=tmp, c2=c2: e.tensor_scalar(out=c2, in0=tmp, scalar1=-4.0, scalar2=None, op0=ALU.mult),
                 reads=[der_v], writes=[der_v])
            for nm, do in (("b_gate_a", 48), ("b_gate_x", 64)):
                src = S[:, PRM_OFF + PCOL[(nm, o)]:PRM_OFF + PCOL[(nm, o)] + 8]
                dst = S[:, DER_OFF + do + o * 8:DER_OFF + do + o * 8 + 8]
                P.op("dve", lambda e, src=src, dst=dst: e.tensor_scalar(out=dst, in0=src, scalar1=0.5, scalar2=None, op0=ALU.mult),
                     reads=[prm_v], writes=[der_v])

        wstate = {"issued": 0, "total": 0}

        def w_slot_view(slot):
            return P.sb(W_OFF + slot * 4096, 4096, r=True)

        def pump(limit):
            while wstate["issued"] < min(limit, wstate["total"]):
                g = wstate["issued"]
                slot = g % R_SLOTS
                t = g % NTILES
                dst = rap("W", W_OFF + slot * 4096, W_OFF + (slot + 1) * 4096)
                P.dma("pool", f"w{slot}", lambda e, dst=dst, t=t: e.dma_start(out=dst, in_=wts[t]),
                      writes=[w_slot_view(slot)])
                wstate["issued"] += 1

        nchunks = NFULL + 1
        wstate["total"] = nchunks * NTILES
        wcur = {"g": 0}

        def next_tile():
            g = wcur["g"]
            pump(g + 1)
            slot = g % R_SLOTS
            wcur["g"] += 1
            return slot

        def tile_done():
            pump(wcur["g"] + R_SLOTS - 1 + 1)

        psrot = {"i": 0}

        def next_bank():
            b = PS_MAIN[psrot["i"] % len(PS_MAIN)]
            psrot["i"] += 1
            return b

        def wl(slot, kc, q, kind):
            base = W_OFF + slot * 4096
            if kind == "in":
                o = base + kc * 512 + q * 128
            else:
                o = base + kc * 128
            return rap("W", o, o + 128)

        def mm_group(bank, T, pairs, reads):
            n = len(pairs)
            out = psum[bank][:, 0:T]
            for i, (l, r) in enumerate(pairs):
                P.op("pe", lambda e, l=l, r=r, i=i: e.matmul(out, l, r, start=(i == 0), stop=(i == n - 1)),
                     reads=reads, writes=[P.ps(bank)], signal=(i == n - 1))

        curx = {"off": X_OFF}

        def x_chunk(kc, T, r=True, off=None):
            if off is None:
                off = curx["off"]
            return P.sb3(off, 8, TP, 0, T, r=r, j0=kc, j1=kc + 1)

        def x_ap(kc, T, r=True, off=None):
            if off is None:
                off = curx["off"]
            a = off + kc * TP
            return rap(RNAME[off], a, a + T) if r else S[:, a:a + T]

        def buf_ap(off, j, width, c0, c1, r=False):
            a = off + j * width
            return rap(RNAME[off], a + c0, a + c1) if r else S[:, a + c0:a + c1]

        def buf_v(off, j, width, r=False):
            return P.sb(off + j * width, width, r=r)

        def grp_dense(ap2d_fn, grp):
            seqs, col0, n, count = grp
            ap = ap2d_fn(col0, col0 + count * n)
            if count > 1:
                ap = ap.rearrange("p (s c) -> p s c", c=n)
            return ap

        def pad_start(groups, gi, H):
            st = 0
            for g in groups[:gi]:
                st += g[3] * (H + g[2])
            return st

        def grp_padded(ap2d_fn, groups, gi, H, shift):
            seqs, col0, n, count = groups[gi]
            ps0 = pad_start(groups, gi, H)
            if count == 1:
                return ap2d_fn(ps0 + shift, ps0 + shift + n)
            ap = ap2d_fn(ps0, ps0 + count * (H + n)).rearrange("p (s c) -> p s c", c=H + n)
            return ap[:, :, shift:shift + n]

        def act_preload_ln():
            dv = P.sb(CST_OFF + 3, 1)
            P.op("act", lambda e: e.activation(out=S[:, CST_OFF + 3:CST_OFF + 4], in_=ONE_AP, func=AF.Ln),
                 reads=[cst_v], writes=[dv])

        def ln_sq(T, j, nslots, get_ap, get_v):
            so = SQ_OFF + (j % nslots) * TP
            sqr = rap("SQ", so, so + T)
            zin = get_ap(j, False)
            P.op("act", lambda e, sqr=sqr, zin=zin: e.activation(out=sqr, in_=zin, func=AF.Square),
                 reads=[get_v(j)], writes=[P.sb(so, TP)])

        def ln_mm(T, j, nch, nslots, get_ap, get_v, ones_off):
            ones_ap = rap("ONES", ones_off, ones_off + 128)
            so = SQ_OFF + (j % nslots) * TP
            sqr = rap("SQ", so, so + T)
            zr = get_ap(j, True)
            P.op("pe", lambda e, zr=zr, j=j: e.matmul(psum[PS_MEAN][:, 0:T], ones_ap, zr, start=(j == 0), stop=(j == nch - 1)),
                 reads=[ones_v, get_v(j)], writes=[P.ps(PS_MEAN)], signal=True)
            P.op("pe", lambda e, sqr=sqr, j=j: e.matmul(psum[PS_EZ2][:, 0:T], ones_ap, sqr, start=(j == 0), stop=(j == nch - 1)),
                 reads=[ones_v, P.sb(so, TP)], writes=[P.ps(PS_EZ2)], signal=True)

        def ln_finish(T, nch, get_ap, get_v, g_name, b_name, func, out_ap, out_v):
            msq = S[:, MSQ_OFF:MSQ_OFF + T]
            msqv = P.sb(MSQ_OFF, TP)
            rstd = S[:, RSTD_OFF:RSTD_OFF + T]
            rstdv = P.sb(RSTD_OFF, TP)
            P.op("act", lambda e: e.activation(out=msq, in_=psum[PS_MEAN][:, 0:T], func=AF.Square),
                 reads=[P.ps(PS_MEAN)], writes=[msqv])
            P.op("dve", lambda e: e.tensor_tensor(out=rstd, in0=psum[PS_EZ2][:, 0:T], in1=msq, op=ALU.subtract),
                 reads=[P.ps(PS_EZ2), msqv], writes=[rstdv])
            P.op("act", lambda e: e.activation(out=rstd, in_=rstd, func=AF.Ln, bias=EPS_AP, scale=1.0),
                 reads=[rstdv, cst_v], writes=[rstdv])
            P.op("act", lambda e: e.activation(out=rstd, in_=rstd, func=AF.Exp, scale=-0.5),
                 reads=[rstdv], writes=[rstdv])
            nearly = min(3, nch)

            def emit_sub(j):
                zj = get_ap(j, False)
                to = TN_OFF + (j % 3) * TP
                tn = S[:, to:to + T]
                P.op("dve", lambda e, zj=zj, tn=tn: e.tensor_tensor(out=tn, in0=zj, in1=psum[PS_MEAN][:, 0:T], op=ALU.subtract),
                     reads=[get_v(j), P.ps(PS_MEAN)], writes=[P.sb(to, TP)])

            def emit_rest(j):
                to = TN_OFF + (j % 3) * TP
                tn = S[:, to:to + T]
                tnv = P.sb(to, TP)
                P.op("dve", lambda e, tn=tn: e.tensor_tensor(out=tn, in0=tn, in1=rstd, op=ALU.mult),
                     reads=[tnv, rstdv], writes=[tnv])
                oj = out_ap(j)
                P.op("act", lambda e, tn=tn, j=j, oj=oj: e.activation(out=oj, in_=tn, func=func,
                                                                      scale=pcol(g_name, j), bias=pcol(b_name, j)),
                     reads=[tnv, prm_v], writes=[out_v(j)])

            for j in range(nearly):
                emit_sub(j)
            for j in range(nch):
                if j >= nearly:
                    emit_sub(j)
                emit_rest(j)

        pend = {"mlp": None, "in": None}

        def ln_partA(T, get_ap, get_v, g_name, gt_off, pbi):
            rn = RNAME[gt_off]
            msq = S[:, MSQ_OFF:MSQ_OFF + T]
            msqv = P.sb(MSQ_OFF, TP)
            rstd = S[:, RSTD_OFF:RSTD_OFF + T]
            rstdv = P.sb(RSTD_OFF, TP)
            P.op("act", lambda e: e.activation(out=msq, in_=psum[PS_MEAN][:, 0:T], func=AF.Square),
                 reads=[P.ps(PS_MEAN)], writes=[msqv])
            for j in range(8):
                d2 = rap(rn, gt_off + j * TP + T, gt_off + j * TP + T + 2)
                po = PB_LOC(pbi) + 2 * j
                s2 = S[:, po:po + 2]
                P.op("dve", lambda e, d2=d2, s2=s2: e.tensor_copy(out=d2, in_=s2), reads=[P.sb(PB_LOC(pbi), 16)], writes=[P.sb(gt_off + j * TP, TP)])
            for j in range(8):
                zj = get_ap(j, False)
                to = TN_OFF + (j % 3) * TP
                tn = S[:, to:to + T]
                tnv = P.sb(to, TP)
                P.op("dve", lambda e, zj=zj, tn=tn: e.tensor_tensor(out=tn, in0=zj, in1=psum[PS_MEAN][:, 0:T], op=ALU.subtract),
                     reads=[get_v(j), P.ps(PS_MEAN)], writes=[tnv])
                gtj = rap(rn, gt_off + j * TP, gt_off + j * TP + T)
                gcol = pcol(g_name, j)
                P.op("dve", lambda e, gtj=gtj, tn=tn, gcol=gcol: e.tensor_scalar(out=gtj, in0=tn, scalar1=gcol, scalar2=None, op0=ALU.mult),
                     reads=[tnv, prm_v], writes=[P.sb(gt_off + j * TP, TP)])
                if j == 0:
                    P.op("dve", lambda e: e.tensor_tensor(out=rstd, in0=psum[PS_EZ2][:, 0:T], in1=msq, op=ALU.subtract),
                         reads=[P.ps(PS_EZ2), msqv], writes=[rstdv])
                    P.op("act", lambda e: e.activation(out=rstd, in_=rstd, func=AF.Ln, bias=EPS_AP, scale=1.0),
                         reads=[rstdv, cst_v], writes=[rstdv])
                    P.op("act", lambda e: e.activation(out=rstd, in_=rstd, func=AF.Exp, scale=-0.5),
                         reads=[rstdv], writes=[rstdv])

        def ln_partB(T, js, gt_off, b_name, out_ap, out_v):
            rstd = S[:, RSTD_OFF:RSTD_OFF + T]
            rstdv = P.sb(RSTD_OFF, TP)
            for j in js:
                gtf = S[:, gt_off + j * TP:gt_off + j * TP + T]
                to = TN_OFF + (j % 3) * TP
                tn = S[:, to:to + T]
                tnv = P.sb(to, TP)
                P.op("dve", lambda e, gtf=gtf, tn=tn: e.tensor_tensor(out=tn, in0=gtf, in1=rstd, op=ALU.mult),
                     reads=[P.sb(gt_off + j * TP, TP), rstdv], writes=[tnv])
                oj = out_ap(j)
                bcol = pcol(b_name, j)
                P.op("act", lambda e, tn=tn, oj=oj, bcol=bcol: e.activation(out=oj, in_=tn, func=AF.Identity, bias=bcol, scale=ONE_AP),
                     reads=[tnv, prm_v, cst_v], writes=[out_v(j)])

        def defer_evac(bank, T, q, tmp_ap, tmp_v, bias_col):
            rstd = S[:, RSTD_OFF:RSTD_OFF + T]
            rstdv = P.sb(RSTD_OFF, TP)
            P.op("dve", lambda e: e.tensor_tensor(out=tmp_ap, in0=psum[bank][:, 0:T], in1=rstd, op=ALU.mult),
                 reads=[P.ps(bank), rstdv], writes=[tmp_v])
            bs2 = S[:, BS_OFF + 2 * q:BS_OFF + 2 * q + 2]
            if bias_col is not None:
                P.op("dve", lambda e: e.tensor_scalar(out=bs2, in0=psum[bank][:, T:T + 2], scalar1=bias_col, scalar2=None, op0=ALU.add),
                     reads=[P.ps(bank), prm_v], writes=[bs_v])
            else:
                P.op("dve", lambda e: e.tensor_copy(out=bs2, in_=psum[bank][:, T:T + 2]), reads=[P.ps(bank)], writes=[bs_v])
            return S[:, BS_OFF + 2 * q:BS_OFF + 2 * q + 1]

        def gt_rhs(gt_off, T):
            rn = RNAME[gt_off]
            return (lambda kc: rap(rn, gt_off + kc * TP, gt_off + kc * TP + T + 2)), (lambda kc: P.sb(gt_off + kc * TP, TP))

        def hist_in(off, nch, width, groups, H, st_off, idx, r=False):
            for gi, (seqs, col0, n, count) in enumerate(groups):
                ps0 = pad_start(groups, gi, H)
                for si, s in enumerate(seqs):
                    c0 = ps0 + si * (H + n)
                    base = rap(RNAME[off], off, off + nch * width) if r else S[:, off:off + nch * width]
                    dst = base.rearrange("p (j c) -> p j c", c=width)[:, :, c0:c0 + H]
                    sb0 = st_off + idx * nch * NSEQ * H
                    src = S[:, sb0:sb0 + nch * NSEQ * H].rearrange("p (j s h) -> p j s h", s=NSEQ, h=H)[:, :, s, :]
                    P.op("dve", lambda e, dst=dst, src=src: e.tensor_copy(out=dst, in_=src),
                         reads=[st_v], writes=[P.sb(off, nch * width)])

        def hist_out(off, nch, width, groups, H, st_off, idx):
            for gi, (seqs, col0, n, count) in enumerate(groups):
                ps0 = pad_start(groups, gi, H)
                for si, s in enumerate(seqs):
                    c0 = ps0 + si * (H + n) + n
                    src = S[:, off:off + nch * width].rearrange("p (j c) -> p j c", c=width)[:, :, c0:c0 + H]
                    sb0 = st_off + idx * nch * NSEQ * H
                    dst = S[:, sb0:sb0 + nch * NSEQ * H].rearrange("p (j s h) -> p j s h", s=NSEQ, h=H)[:, :, s, :]
                    P.op("dve", lambda e, dst=dst, src=src: e.tensor_copy(out=dst, in_=src),
                         reads=[P.sb(off, nch * width)], writes=[st_v])

        def dwconv(in_off, nch, width, groups, H, ntaps, w_name, bias_name, acc_off, fin_off, fin_r, only=None, js=None):
            for k in range(ntaps):
                last = (k == ntaps - 1)
                for j in (js if js is not None else range(nch)):
                    for gi, grp in enumerate(groups):
                        if only is not None and gi not in only:
                            continue
                        src = grp_padded(lambda a, b, j=j: buf_ap(in_off, j, width, a, b), groups, gi, H, k)
                        if last:
                            dst = grp_dense(lambda a, b, j=j: buf_ap(fin_off, j, TP, a, b, fin_r), grp)
                            dstv = buf_v(fin_off, j, TP)
                        else:
                            dst = grp_dense(lambda a, b, j=j: buf_ap(acc_off, j, TP, a, b), grp)
                            dstv = buf_v(acc_off, j, TP)
                        wk = pcol(w_name, j * ntaps + k)
                        if k == 0:
                            if bias_name is not None:
                                P.op("dve", lambda e, dst=dst, src=src, wk=wk, j=j: e.tensor_scalar(
                                    out=dst, in0=src, scalar1=wk, scalar2=pcol(bias_name, j), op0=ALU.mult, op1=ALU.add),
                                    reads=[buf_v(in_off, j, width), prm_v], writes=[dstv])
                            else:
                                P.op("dve", lambda e, dst=dst, src=src, wk=wk: e.tensor_scalar(
                                    out=dst, in0=src, scalar1=wk, scalar2=None, op0=ALU.mult),
                                    reads=[buf_v(in_off, j, width), prm_v], writes=[dstv])
                        else:
                            acc = grp_dense(lambda a, b, j=j: buf_ap(acc_off, j, TP, a, b), grp)
                            P.op("dve", lambda e, dst=dst, src=src, wk=wk, acc=acc: e.scalar_tensor_tensor(
                                out=dst, in0=src, scalar=wk, in1=acc, op0=ALU.mult, op1=ALU.add),
                                reads=[buf_v(in_off, j, width), prm_v, buf_v(acc_off, j, TP)], writes=[dstv])

        def tile_kouter(slot, T, rhs_ap, rhs_v, evac, ncols=None):
            banks = [next_bank() for _ in range(4)]
            ncols = T if ncols is None else ncols
            for kc in range(8):
                for q in range(4):
                    l_ = wl(slot, kc, q, "in")
                    r_ = rhs_ap(kc)
                    P.op("pe", lambda e, kc=kc, q=q, l_=l_, r_=r_: e.matmul(psum[banks[q]][:, 0:ncols], l_, r_,
                                                                            start=(kc == 0), stop=(kc == 7)),
                         reads=[w_slot_view(slot), rhs_v(kc)], writes=[P.ps(banks[q])], signal=(kc == 7 or q == 3))
            for q in range(4):
                evac(q, banks[q])

        def pe_diag_conv(nch, ntaps, in_off, width, H, n0, w_name, bias_name, fin_off, k0=0, acc_off=None, evac="act"):
            if k0 > 0:
                for k in range(k0):
                    for j in range(nch):
                        src = buf_ap(in_off, j, width, k, k + n0)
                        acc = buf_ap(acc_off, j, TP, 0, n0)
                        wk = pcol(w_name, j * ntaps + k)
                        if k == 0:
                            P.op("dve", lambda e, acc=acc, src=src, wk=wk, j=j: e.tensor_scalar(
                                out=acc, in0=src, scalar1=wk, scalar2=pcol(bias_name, j), op0=ALU.mult, op1=ALU.add),
                                reads=[buf_v(in_off, j, width), prm_v], writes=[buf_v(acc_off, j, TP)])
                        else:
                            P.op("dve", lambda e, acc=acc, src=src, wk=wk: e.scalar_tensor_tensor(
                                out=acc, in0=src, scalar=wk, in1=acc, op0=ALU.mult, op1=ALU.add),
                                reads=[buf_v(in_off, j, width), prm_v, buf_v(acc_off, j, TP)], writes=[buf_v(acc_off, j, TP)])
            ntp = ntaps - k0
            nmat = nch * ntp
            slot = None
            for m in range(nmat):
                if m % 32 == 0:
                    if slot is not None:
                        tile_done()
                    slot = next_tile()
                j, kk = divmod(m, ntp)
                k = k0 + kk
                if kk == 0:
                    bank = next_bank()
                o = W_OFF + slot * 4096 + (m % 32) * 128
                lhs = rap("W", o, o + 128)
                rhs = buf_ap(in_off, j, width, k, k + n0, True)
                P.op("pe", lambda e, lhs=lhs, rhs=rhs, bank=bank, kk=kk: e.matmul(psum[bank][:, 0:n0], lhs, rhs, start=(kk == 0), stop=(kk == ntp - 1)),
                     reads=[w_slot_view(slot), buf_v(in_off, j, width)], writes=[P.ps(bank)], signal=(kk == ntp - 1))
                if kk == ntp - 1:
                    dst = buf_ap(fin_off, j, TP, 0, n0, True)
                    if k0 > 0:
                        acc = buf_ap(acc_off, j, TP, 0, n0)
                        P.op("dve", lambda e, dst=dst, bank=bank, acc=acc: e.tensor_tensor(out=dst, in0=psum[bank][:, 0:n0], in1=acc, op=ALU.add),
                             reads=[P.ps(bank), buf_v(acc_off, j, TP)], writes=[buf_v(fin_off, j, TP)])
                    elif evac == "dve":
                        P.op("dve", lambda e, dst=dst, bank=bank, j=j: e.tensor_scalar(out=dst, in0=psum[bank][:, 0:n0], scalar1=pcol(bias_name, j),
                                                                                      scalar2=None, op0=ALU.add),
                             reads=[P.ps(bank), prm_v], writes=[buf_v(fin_off, j, TP)])
                    else:
                        P.op("act", lambda e, dst=dst, bank=bank, j=j: e.activation(out=dst, in_=psum[bank][:, 0:n0], func=AF.Identity,
                                                                                   bias=pcol(bias_name, j), scale=1.0),
                             reads=[P.ps(bank), prm_v], writes=[buf_v(fin_off, j, TP)])
            tile_done()

        def x_get(T):
            return (lambda j, r: x_ap(j, T, r)), (lambda j: x_chunk(j, T))

        def out_proj_and_ln1(l, T, y_off, zt_off, bname):
            xg, xv = x_get(T)
            lnmm = {"n": 0}
            for g in range(2):
                slot = next_tile()
                if g == 0:
                    g0banks = []
                    tile_kouter(slot, T, lambda kc: x_ap(kc, T, True, y_off), lambda kc: x_chunk(kc, T, True, y_off),
                                lambda q, bank: g0banks.append(bank))
                for q in range(4):
                    oc = g * 4 + q
                    if g == 0:
                        bank = g0banks[q]
                    else:
                        bank = next_bank()
                        mm_group(bank, T, [(wl(slot, kc, q, "in"), x_ap(kc, T, True, y_off)) for kc in range(8)],
                                 reads=[w_slot_view(slot)] + [x_chunk(kc, T, True, y_off) for kc in range(8)])
                    if g == 1:
                        while lnmm["n"] <= oc - 2:
                            ln_mm(T, lnmm["n"], 8, 4, xg, xv, ONES_OFF)
                            lnmm["n"] += 1
                    zt = S[:, zt_off + (oc % 2) * TP:zt_off + (oc % 2) * TP + T]
                    ztv = P.sb(zt_off + (oc % 2) * TP, TP)
                    P.op("act", lambda e, zt=zt, bank=bank, oc=oc: e.activation(out=zt, in_=psum[bank][:, 0:T], func=AF.Identity,
                                                                               bias=pcol(bname, oc), scale=1.0),
                         reads=[P.ps(bank), prm_v], writes=[ztv])
                    if oc >= 1:
                        ln_sq(T, oc - 1, 4, xg, xv)
                    xo = x_ap(oc, T, True)
                    xf = x_ap(oc, T, False)
                    P.op("dve", lambda e, zt=zt, xo=xo, xf=xf: e.scalar_tensor_tensor(out=xo, in0=xf, scalar=ALPHA, in1=zt,
                                                                              op0=ALU.mult, op1=ALU.add),
                         reads=[ztv, x_chunk(oc, T)], writes=[x_chunk(oc, T)])
                tile_done()
            ln_sq(T, 7, 4, xg, xv)
            ln_mm(T, 6, 8, 4, xg, xv, ONES_OFF)
            ln_mm(T, 7, 8, 4, xg, xv, ONES_OFF)
            ln_partA(T, xg, xv, ("ln1_g", l), curx["off"], l)

        def mlp_and_ln2(l, T, last):
            xo_ap = lambda j: x_ap(j, T, True)
            xo_v = lambda j: x_chunk(j, T)
            for g in range(8):
                slot = next_tile()
                if g == 0:
                    g0banks = []
                    gr, gv = gt_rhs(curx["off"], T)
                    tile_kouter(slot, T, gr, gv, lambda q, bank: g0banks.append(bank), ncols=T + 2)
                    ln_partB(T, range(0, 4), curx["off"], ("ln1_b", l), xo_ap, xo_v)
                for q in range(4):
                    f = g * 4 + q
                    zt = S[:, M_ZT + (f % 2) * TP:M_ZT + (f % 2) * TP + T]
                    ztv = P.sb(M_ZT + (f % 2) * TP, TP)
                    if g == 0:
                        bank = g0banks[q]
                        bs = defer_evac(bank, T, q, zt, ztv, None)
                        P.op("act", lambda e, zt=zt, bs=bs: e.activation(out=zt, in_=zt, func=AF.Relu, bias=bs, scale=ONE_AP),
                             reads=[ztv, bs_v, cst_v], writes=[ztv])
                    else:
                        bank = next_bank()
                        mm_group(bank, T, [(wl(slot, kc, q, "in"), x_ap(kc, T)) for kc in range(8)],
                                 reads=[w_slot_view(slot)] + [x_chunk(kc, T) for kc in range(8)])
                        P.op("act", lambda e, zt=zt, bank=bank: e.activation(out=zt, in_=psum[bank][:, 0:T], func=AF.Relu),
                             reads=[P.ps(bank)], writes=[ztv])
                    hf = rap("HID", M_HID + f * TP, M_HID + f * TP + T)
                    P.op("dve", lambda e, zt=zt, hf=hf: e.tensor_tensor(out=hf, in0=zt, in1=zt, op=ALU.mult),
                         reads=[ztv], writes=[P.sb(M_HID + f * TP, TP)])
                if g == 0:
                    ln_partB(T, range(4, 8), curx["off"], ("ln1_b", l), xo_ap, xo_v)
                tile_done()
            xg, xv = x_get(T)
            for oc in range(8):
                slot = next_tile()
                bank = next_bank()
                mm_group(bank, T, [(wl(slot, kc, 0, "m2"), rap("HID", M_HID + kc * TP, M_HID + kc * TP + T)) for kc in range(32)],
                         reads=[w_slot_view(slot), P.sb(M_HID, 32 * TP)])
                if oc >= 1:
                    ln_mm(T, oc - 1, 8, 4, xg, xv, ONES_OFF)
                xo = x_ap(oc, T, True)
                xf = x_ap(oc, T, False)
                P.op("dve", lambda e, xo=xo, xf=xf, bank=bank: e.scalar_tensor_tensor(out=xo, in0=xf, scalar=ALPHA,
                                                                              in1=psum[bank][:, 0:T], op0=ALU.mult, op1=ALU.add),
                     reads=[P.ps(bank), x_chunk(oc, T)], writes=[x_chunk(oc, T)])
                ln_sq(T, oc, 4, xg, xv)
                tile_done()
            ln_mm(T, 7, 8, 4, xg, xv, ONES_OFF)
            if last:
                ln_finish(T, 8, xg, xv, ("ln2_g", l), ("ln2_b", l), AF.Identity,
                          lambda j: x_ap(j, T, False, YO_OFF), lambda j: x_chunk(j, T, False, YO_OFF))
            else:
                if DEFER_IN:
                    gto = curx["off"]
                    ln_partA(T, xg, xv, ("ln2_g", l), gto, 4 + l)
                    pend["in"] = (gto, ("ln2_b", l))
                else:
                    ln_finish(T, 8, xg, xv, ("ln2_g", l), ("ln2_b", l), AF.Identity,
                              lambda j: x_ap(j, T, True), lambda j: x_chunk(j, T))

        def even_layer(l, T, groups):
            e_ = l // 2
            nA = groups[0][2]
            x_reads = [x_chunk(kc, T) for kc in range(8)]
            hist_in(E_UBUF, 4, WU, groups, 30, SA_OFF + ST_OFF, e_, r=True)
            hist_in(E_VBUF, 4, WV, groups, 2, SB_OFF + ST_OFF, e_)

            def in_tile(evac):
                slot = next_tile()
                for q in range(4):
                    bank = next_bank()
                    mm_group(bank, T, [(wl(slot, kc, q, "in"), x_ap(kc, T)) for kc in range(8)],
                             reads=[w_slot_view(slot)] + x_reads)
                    evac(q, bank)
                tile_done()

            def ev_gate(q, bank):
                sg = S[:, E_SG + q * TP:E_SG + q * TP + T]
                P.op("act", lambda e: e.activation(out=sg, in_=psum[bank][:, 0:T], func=AF.Sigmoid,
                                                   bias=pcol(("b_in_e", e_), 4 + q), scale=1.0),
                     reads=[P.ps(bank), prm_v], writes=[P.sb(E_SG + q * TP, TP)])
            slot0 = next_tile()
            if pend["in"] is None:
                tile_kouter(slot0, T, lambda kc: x_ap(kc, T), lambda kc: x_chunk(kc, T), ev_gate)
            else:
                gto, bnm = pend["in"]
                pend["in"] = None
                gr, gv = gt_rhs(gto, T)
                g0banks = []
                tile_kouter(slot0, T, gr, gv, lambda q, bank: g0banks.append(bank), ncols=T + 2)
                xo_ap = lambda j: x_ap(j, T, True)
                xo_v = lambda j: x_chunk(j, T)
                ln_partB(T, range(0, 4), gto, bnm, xo_ap, xo_v)
                for q in range(4):
                    sg = S[:, E_SG + q * TP:E_SG + q * TP + T]
                    sgv = P.sb(E_SG + q * TP, TP)
                    bs = defer_evac(g0banks[q], T, q, sg, sgv, pcol(("b_in_e", e_), 4 + q))
                    P.op("act", lambda e, sg=sg, bs=bs: e.activation(out=sg, in_=sg, func=AF.Sigmoid, bias=bs, scale=ONE_AP),
                         reads=[sgv, bs_v, cst_v], writes=[sgv])
                ln_partB(T, range(4, 8), gto, bnm, xo_ap, xo_v)
            tile_done()
            act_preload_ln()

            def ev_val(q, bank):
                for gi, grp in enumerate(groups):
                    dst = grp_padded(lambda a, b: buf_ap(E_UBUF, q, WU, a, b, True), groups, gi, 30, 30)
                    src = grp_dense(lambda a, b: psum[bank][:, a:b], grp)
                    sgg = grp_dense(lambda a, b: S[:, E_SG + q * TP + a:E_SG + q * TP + b], grp)
                    P.op("dve", lambda e, dst=dst, src=src, sgg=sgg: e.scalar_tensor_tensor(
                        out=dst, in0=src, scalar=pcol(("b_in_e", e_), q), in1=sgg, op0=ALU.add, op1=ALU.mult),
                        reads=[P.ps(bank), prm_v, P.sb(E_SG + q * TP, TP)], writes=[buf_v(E_UBUF, q, WU)])
            in_tile(ev_val)

            pe_diag_conv(4, 31, E_UBUF, WU, 30, nA, ("conv_a_w", e_), ("conv_a_b", e_), E_CA, k0=K0A, acc_off=E_ACC)
            if len(groups) > 1:
                dwconv(E_UBUF, 4, WU, groups, 30, 31, ("conv_a_w", e_), ("conv_a_b", e_), E_ACC, E_CA, True, only=[1])
            hist_out(E_UBUF, 4, WU, groups, 30, SA_OFF + ST_OFF, e_)
            cag = lambda j, r: buf_ap(E_CA, j, TP, 0, T, r)
            cav = lambda j: buf_v(E_CA, j, TP)
            for j in range(4):
                ln_sq(T, j, 4, cag, cav)

            def ev_hb(q, bank):
                hb = S[:, E_HB + q * TP:E_HB + q * TP + T]
                P.op("act", lambda e: e.activation(out=hb, in_=psum[bank][:, 0:T], func=AF.Identity,
                                                   bias=pcol(("b_in_e", e_), 16 + q), scale=1.0),
                     reads=[P.ps(bank), prm_v], writes=[P.sb(E_HB + q * TP, TP)])
            in_tile(ev_hb)
            for j in range(4):
                ln_mm(T, j, 4, 4, cag, cav, ONES_OFF + 128)
            ln_finish(T, 4, cag, cav, ("ln_a_g", e_), ("ln_a_b", e_), AF.Silu,
                      lambda j: buf_ap(E_Y, j, TP, 0, T, True), lambda j: buf_v(E_Y, j, TP))

            def ev_gc(q, bank):
                for gi, grp in enumerate(groups):
                    dst = grp_padded(lambda a, b: buf_ap(E_VBUF, q, WV, a, b), groups, gi, 2, 2)
                    src = grp_dense(lambda a, b: psum[bank][:, a:b], grp)
                    hbg = grp_dense(lambda a, b: S[:, E_HB + q * TP + a:E_HB + q * TP + b], grp)
                    P.op("dve", lambda e, dst=dst, src=src, hbg=hbg: e.scalar_tensor_tensor(
                        out=dst, in0=src, scalar=pcol(("b_in_e", e_), 12 + q), in1=hbg, op0=ALU.add, op1=ALU.mult),
                        reads=[P.ps(bank), prm_v, P.sb(E_HB + q * TP, TP)], writes=[buf_v(E_VBUF, q, WV)])
            in_tile(ev_gc)

            def ev_gb(q, bank):
                gb = S[:, E_GB + q * TP:E_GB + q * TP + T]
                P.op("act", lambda e: e.activation(out=gb, in_=psum[bank][:, 0:T], func=AF.Identity,
                                                   bias=pcol(("b_in_e", e_), 8 + q), scale=1.0),
                     reads=[P.ps(bank), prm_v], writes=[P.sb(E_GB + q * TP, TP)])
            in_tile(ev_gb)

            act_preload_ln()
            dwconv(E_VBUF, 4, WV, groups, 2, 3, ("conv_b_w", e_), None, E_ACC, E_ACC, False)
            hist_out(E_VBUF, 4, WV, groups, 2, SB_OFF + ST_OFF, e_)
            for j in range(4):
                yb = buf_ap(E_Y, 4 + j, TP, 0, T, True)
                cb = buf_ap(E_ACC, j, TP, 0, T)
                gb = S[:, E_GB + j * TP:E_GB + j * TP + T]
                P.op("dve", lambda e, yb=yb, gb=gb, cb=cb: e.tensor_tensor(out=yb, in0=cb, in1=gb, op=ALU.mult),
                     reads=[buf_v(E_ACC, j, TP), P.sb(E_GB + j * TP, TP)], writes=[buf_v(E_Y, 4 + j, TP)])
            out_proj_and_ln1(l, T, E_Y, E_ZT, ("b_out_e", e_))

        def odd_layer(l, T, groups):
            o_ = l // 2
            nA = groups[0][2]
            x_reads = [x_chunk(kc, T) for kc in range(8)]
            hist_in(O_CBUF, 8, WC, groups, 3, SC_OFF + ST_OFF, o_, r=True)
            O_GGF = SQ_OFF
            QUART = S[:, DER_OFF + 80:DER_OFF + 81]
            tbase = wcur["g"]
            sU = [next_tile(), next_tile()]
            sC = next_tile()

            def emit_u_evac(h, bank):
                for gi, grp in enumerate(groups):
                    dst = grp_padded(lambda a, b: buf_ap(O_CBUF, h, WC, a, b, True), groups, gi, 3, 3)
                    src = grp_dense(lambda a, b: psum[bank][:, a:b], grp)
                    P.op("act", lambda e, dst=dst, src=src, h=h: e.activation(out=dst, in_=src, func=AF.Identity,
                                                                              bias=pcol(("b_in_o", o_), 8 + h), scale=1.0),
                         reads=[P.ps(bank), prm_v], writes=[buf_v(O_CBUF, h, WC)])

            def emit_u(h):
                bank = next_bank()
                mm_group(bank, T, [(wl(sU[1], kc, h - 4, "in"), x_ap(kc, T)) for kc in range(8)],
                         reads=[w_slot_view(sU[1])] + x_reads)
                emit_u_evac(h, bank)

            def emit_conv(h):
                bank = next_bank()
                for k in range(4):
                    o = W_OFF + sC * 4096 + (h * 4 + k) * 128
                    lhs = rap("W", o, o + 128)
                    rhs = buf_ap(O_CBUF, h, WC, k, k + nA, True)
                    P.op("pe", lambda e, lhs=lhs, rhs=rhs, bank=bank, k=k: e.matmul(psum[bank][:, 0:nA], lhs, rhs, start=(k == 0), stop=(k == 3)),
                         reads=[w_slot_view(sC), buf_v(O_CBUF, h, WC)], writes=[P.ps(bank)], signal=(k == 3))
                dst = buf_ap(O_XC, h, TP, 0, nA, True)
                P.op("dve", lambda e, dst=dst, bank=bank, h=h: e.tensor_scalar(out=dst, in0=psum[bank][:, 0:nA], scalar1=pcol(("conv_c_b", o_), h),
                                                                              scalar2=None, op0=ALU.add),
                     reads=[P.ps(bank), prm_v], writes=[buf_v(O_XC, h, TP)])
                if len(groups) > 1:
                    dwconv(O_CBUF, 8, WC, groups, 3, 4, ("conv_c_w", o_), ("conv_c_b", o_), O_A2M, O_XC, True, only=[1], js=[h])

            def emit_gates(h):
                bank = next_bank()
                go = GW_OFF + ((o_ * 2 + 0) * 8 + h) * 128
                mm_group(bank, T, [(rap("GW", go, go + 128), buf_ap(O_XC, h, TP, 0, T, True))], reads=[gw_v, buf_v(O_XC, h, TP)])
                bank2 = next_bank()
                go2 = GW_OFF + ((o_ * 2 + 1) * 8 + h) * 128
                mm_group(bank2, T, [(rap("GW", go2, go2 + 128), buf_ap(O_XC, h, TP, 0, T, True))], reads=[gw_v, buf_v(O_XC, h, TP)])
                ra = buf_ap(O_RA, h, TP, 0, T)
                a2 = buf_ap(O_A2M, h, TP, 0, T)
                ib = buf_ap(O_IB, h, TP, 0, T)
                xc = buf_ap(O_XC, h, TP, 0, T)
                hba = S[:, DER_OFF + 48 + o_ * 8 + h:DER_OFF + 48 + o_ * 8 + h + 1]
                hbx = S[:, DER_OFF + 64 + o_ * 8 + h:DER_OFF + 64 + o_ * 8 + h + 1]
                hc = S[:, DER_OFF + 16 + o_ * 8 + h:DER_OFF + 16 + o_ * 8 + h + 1]
                P.op("act", lambda e: e.activation(out=ra, in_=psum[bank][:, 0:T], func=AF.Tanh, bias=hba, scale=0.5),
                     reads=[P.ps(bank), der_v], writes=[buf_v(O_RA, h, TP)])
                P.op("act", lambda e: e.activation(out=ib, in_=psum[bank2][:, 0:T], func=AF.Tanh, bias=hbx, scale=0.5),
                     reads=[P.ps(bank2), der_v], writes=[buf_v(O_IB, h, TP)])
                P.op("act", lambda e: e.activation(out=ra, in_=ra, func=AF.Exp, scale=hc, bias=hc),
                     reads=[buf_v(O_RA, h, TP), der_v], writes=[buf_v(O_RA, h, TP)])
                P.op("dve", lambda e: e.scalar_tensor_tensor(out=ib, in0=ib, scalar=1.0, in1=xc, op0=ALU.add, op1=ALU.mult),
                     reads=[buf_v(O_XC, h, TP), buf_v(O_IB, h, TP)], writes=[buf_v(O_IB, h, TP)])
                P.op("dve", lambda e: e.tensor_tensor(out=a2, in0=ra, in1=ra, op=ALU.mult),
                     reads=[buf_v(O_RA, h, TP)], writes=[buf_v(O_A2M, h, TP)])
                P.op("dve", lambda e: e.tensor_scalar(out=a2, in0=a2, scalar1=1.0, scalar2=-1.0, op0=ALU.min, op1=ALU.mult),
                     reads=[buf_v(O_A2M, h, TP)], writes=[buf_v(O_A2M, h, TP)])

            if pend["in"] is None:
                tile_kouter(sU[0], T, lambda kc: x_ap(kc, T), lambda kc: x_chunk(kc, T), emit_u_evac)
            else:
                gto, bnm = pend["in"]
                pend["in"] = None
                gr, gv = gt_rhs(gto, T)
                g0banks = []
                tile_kouter(sU[0], T, gr, gv, lambda q, bank: g0banks.append(bank), ncols=T + 2)
                xo_ap = lambda j: x_ap(j, T, True)
                xo_v = lambda j: x_chunk(j, T)
                ln_partB(T, range(0, 4), gto, bnm, xo_ap, xo_v)
                for q in range(4):
                    tmp = S[:, O_RA + q * TP:O_RA + q * TP + T]
                    tmpv = P.sb(O_RA + q * TP, TP)
                    bs = defer_evac(g0banks[q], T, q, tmp, tmpv, pcol(("b_in_o", o_), 8 + q))
                    for gi, grp in enumerate(groups):
                        dst = grp_padded(lambda a, b: buf_ap(O_CBUF, q, WC, a, b, True), groups, gi, 3, 3)
                        src = grp_dense(lambda a, b: S[:, O_RA + q * TP + a:O_RA + q * TP + b], grp)
                        P.op("act", lambda e, dst=dst, src=src, bs=bs: e.activation(out=dst, in_=src, func=AF.Identity, bias=bs, scale=ONE_AP),
                             reads=[tmpv, bs_v, cst_v], writes=[buf_v(O_CBUF, q, WC)])
                ln_partB(T, range(4, 8), gto, bnm, xo_ap, xo_v)
            pump(tbase + 4)
            for step in range(9):
                if step < 8:
                    emit_conv(step)
                if step < 4:
                    emit_u(4 + step)
                    if step == 3:
                        pump(tbase + 5)
                        hist_out(O_CBUF, 8, WC, groups, 3, SC_OFF + ST_OFF, o_)
                if step >= 1:
                    emit_gates(step - 1)
            pump(tbase + 6)
            for h in range(8):
                a2 = buf_ap(O_A2M, h, TP, 0, T)
                P.op("act", lambda e, a2=a2: e.activation(out=a2, in_=a2, func=AF.Sqrt, bias=QUART, scale=0.25),
                     reads=[buf_v(O_A2M, h, TP), der_v], writes=[buf_v(O_A2M, h, TP)])
            for g in range(2):
                slot = next_tile()
                for q in range(4):
                    j = g * 4 + q
                    bank = next_bank()
                    mm_group(bank, T, [(wl(slot, kc, q, "in"), x_ap(kc, T)) for kc in range(8)],
                             reads=[w_slot_view(slot)] + x_reads)
                    gg = S[:, O_GGF + j * TP:O_GGF + j * TP + T]
                    P.op("act", lambda e, gg=gg, bank=bank, j=j: e.activation(out=gg, in_=psum[bank][:, 0:T], func=AF.Gelu_apprx_tanh,
                                                                              bias=pcol(("b_in_o", o_), j), scale=1.0),
                         reads=[P.ps(bank), prm_v], writes=[P.sb(O_GGF + j * TP, TP)])
                tile_done()
            act_preload_ln()
            O_HH = O_CBUF
            for h in range(8):
                a2 = buf_ap(O_A2M, h, TP, 0, T)
                ib = buf_ap(O_IB, h, TP, 0, T)
                P.op("dve", lambda e, a2=a2, ib=ib: e.tensor_tensor(out=ib, in0=ib, in1=a2, op=ALU.mult),
                     reads=[buf_v(O_A2M, h, TP), buf_v(O_IB, h, TP)], writes=[buf_v(O_IB, h, TP)])
                for (seqs, col0, n, count) in groups:
                    for si, s in enumerate(seqs):
                        c0 = col0 + si * n
                        sidx = ST_OFF + SH_OFF + (o_ * 8 + h) * NSEQ + s
                        init = S[:, sidx:sidx + 1]
                        P.op("dve", lambda e, h=h, c0=c0, n=n, init=init: e.tensor_tensor_scan(
                            out=buf_ap(O_HH, h, TP, c0, c0 + n), data0=buf_ap(O_RA, h, TP, c0, c0 + n),
                            data1=buf_ap(O_IB, h, TP, c0, c0 + n), initial=init, op0=ALU.mult, op1=ALU.add),
                            reads=[buf_v(O_RA, h, TP), buf_v(O_IB, h, TP), st_v], writes=[buf_v(O_HH, h, TP)])
                gg = S[:, O_GGF + h * TP:O_GGF + h * TP + T]
                hh = buf_ap(O_HH, h, TP, 0, T)
                yr = buf_ap(O_A2M, h, TP, 0, T, True)
                P.op("pool", lambda e, gg=gg, hh=hh, yr=yr: e.tensor_tensor(out=yr, in0=gg, in1=hh, op=ALU.mult),
                     reads=[P.sb(O_GGF + h * TP, TP), buf_v(O_HH, h, TP), buf_v(O_IB, h, TP)], writes=[buf_v(O_A2M, h, TP)])
            for (seqs, col0, n, count) in groups:
                for si, s in enumerate(seqs):
                    cl = col0 + si * n + n - 1
                    src = S[:, O_HH:O_HH + 8 * TP].rearrange("p (j c) -> p j c", c=TP)[:, :, cl:cl + 1]
                    sb0 = ST_OFF + SH_OFF + o_ * 8 * NSEQ
                    dst = S[:, sb0:sb0 + 8 * NSEQ].rearrange("p (j s) -> p j s", s=NSEQ)[:, :, s:s + 1]
                    P.op("dve", lambda e, dst=dst, src=src: e.tensor_copy(out=dst, in_=src),
                         reads=[P.sb(O_HH, 8 * TP)], writes=[st_v])
            out_proj_and_ln1(l, T, O_A2M, O_RA, ("b_out_o", o_))

        colbases = []
        cb_ = 0
        for c in range(nchunks):
            colbases.append(cb_)
            cb_ += chunk_groups(c)[1]

        def load_x(c):
            _, Tc = chunk_groups(c)
            xo_ = X_OFF if c % 2 == 0 else X1_OFF
            src = xin[:, colbases[c]:colbases[c] + Tc].rearrange("(k p) t -> p k t", p=128)
            dst = rap(RNAME[xo_], xo_, xo_ + 8 * TP).rearrange("p (k t) -> p k t", t=TP)[:, :, 0:Tc]
            P.dma("pool", "x" if c % 2 == 0 else "x1", lambda e, dst=dst, src=src: e.dma_start(out=dst, in_=src), writes=[P.sb(xo_, 8 * TP)])

        load_x(0)
        for c in range(nchunks):
            groups, T = chunk_groups(c)
            colbase = colbases[c]
            curx["off"] = X_OFF if c % 2 == 0 else X1_OFF
            if c + 1 < nchunks:
                load_x(c + 1)
            for l in range(DEPTH):
                if l % 2 == 0:
                    even_layer(l, T, groups)
                else:
                    odd_layer(l, T, groups)
                mlp_and_ln2(l, T, last=(l == DEPTH - 1))
            ysrc = S[:, YO_OFF:YO_OFF + 8 * TP].rearrange("p (k t) -> p k t", t=TP)[:, :, 0:T]
            ydst = yout[:, colbase:colbase + T].rearrange("(k p) t -> p k t", p=128)
            P.dma("sp", "y", lambda e, ydst=ydst, ysrc=ysrc: e.dma_start(out=ydst, in_=ysrc), reads=[P.sb(YO_OFF, 8 * TP)])
        assert wcur["g"] == wstate["total"] == wstate["issued"], (wcur, wstate)
        P.dma("sp", "so", lambda e: e.dma_start(out=sto, in_=S[:, ST_OFF:ST_OFF + NST]), reads=[st_v])
        P.wait_all("sp", "y")
        P.wait_all("sp", "so")

        @block.sync
        def _(eng):
            P.replay("sp", eng)

        @block.gpsimd
        def _(eng):
            P.replay("pool", eng)

        @block.tensor
        def _(eng):
            P.replay("pe", eng)

        @block.scalar
        def _(eng):
            P.replay("act", eng)

        @block.vector
        def _(eng):
            P.replay("dve", eng)

    return nc


def _fm(v, nch):
    return np.ascontiguousarray(np.asarray(v, np.float32).reshape(nch, 128).T)


def _prep_shared(inp):
    f32 = np.float32
    wsrc = {k: np.asarray(inp[k], f32) for k in ("w_in_e", "w_out_e", "w_in_o", "w_out_o", "w_mlp1", "w_mlp2")}
    wt = np.empty((NTILES, 128, 4096), f32)
    for t, (name, idx, g) in enumerate(PLAN):
        W = wsrc[name][idx] if name in wsrc else None
        if name == "diag_a":
            w = np.asarray(inp["conv_a_w"], f32)[idx]
            tl = np.zeros((128, 4096), f32)
            pp = np.arange(128)
            ntp = 31 - K0A
            for mm in range(g * 32, min(g * 32 + 32, 4 * ntp)):
                j, k = divmod(mm, ntp)
                tl[pp, (mm % 32) * 128 + pp] = w[K0A + k, j * 128:(j + 1) * 128]
            wt[t] = tl
            continue
        if name == "diag_c":
            w = np.asarray(inp["conv_c_w"], f32)[idx]
            tl = np.zeros((128, 4096), f32)
            pp = np.arange(128)
            for mm in range(32):
                j, k = divmod(mm, 4)
                tl[pp, mm * 128 + pp] = w[k, j * 128:(j + 1) * 128]
            wt[t] = tl
            continue
        if name == "w_mlp2":
            blk = W[:, g * 128:(g + 1) * 128]
            wt[t] = blk.reshape(32, 128, 128).transpose(1, 0, 2).reshape(128, 4096)
        else:
            blk = W[:, g * 512:(g + 1) * 512]
            wt[t] = blk.reshape(8, 128, 512).transpose(1, 0, 2).reshape(128, 4096)
    gw = np.empty((128, 2, 2, 8, 128), f32)
    for o in range(2):
        gw[:, o, 0] = np.asarray(inp["w_gate_a"], f32)[o].transpose(1, 0, 2)
        gw[:, o, 1] = np.asarray(inp["w_gate_x"], f32)[o].transpose(1, 0, 2)
    gw = gw.reshape(128, 4096)
    prm = np.zeros((128, NPRM), f32)

    def put(name, arr):
        prm[:, PCOL[name]:PCOL[name] + arr.shape[1]] = arr

    for l in range(DEPTH):
        for nm in ("ln1_g", "ln1_b", "ln2_g", "ln2_b"):
            put((nm, l), _fm(inp[nm][l], 8))
    for e in range(2):
        put(("b_in_e", e), _fm(inp["b_in_e"][e], 20))
        w = np.asarray(inp["conv_a_w"], f32)[e]
        put(("conv_a_w", e), w.reshape(31, 4, 128).transpose(2, 1, 0).reshape(128, 124))
        put(("conv_a_b", e), _fm(inp["conv_a_b"][e], 4))
        put(("ln_a_g", e), _fm(inp["ln_a_g"][e], 4))
        put(("ln_a_b", e), _fm(inp["ln_a_b"][e], 4))
        w = np.asarray(inp["conv_b_w"], f32)[e]
        put(("conv_b_w", e), w.reshape(3, 4, 128).transpose(2, 1, 0).reshape(128, 12))
        put(("b_out_e", e), _fm(inp["b_out_e"][e], 8))
    for o in range(2):
        put(("b_in_o", o), _fm(inp["b_in_o"][o], 16))
        w = np.asarray(inp["conv_c_w"], f32)[o]
        put(("conv_c_w", o), w.reshape(4, 8, 128).transpose(2, 1, 0).reshape(128, 32))
        put(("conv_c_b", o), _fm(inp["conv_c_b"][o], 8))
        put(("b_gate_a", o), _fm(inp["b_gate_a"][o], 8))
        put(("b_gate_x", o), _fm(inp["b_gate_x"][o], 8))
        put(("lru_lambda", o), _fm(inp["lru_lambda"][o], 8))
        put(("b_out_o", o), _fm(inp["b_out_o"][o], 8))
    return wt, gw, prm


def _prep_core(inp, b):
    f32 = np.float32
    xin = np.empty((D, NTOK), f32)
    xin[:, :NMETA] = np.asarray(inp["meta_tokens"], f32).T
    xin[:, NMETA:NPROMPT] = np.asarray(inp["x_prompt"][b], f32).T
    xs = np.asarray(inp["x_sample"][NSAMP * b:NSAMP * (b + 1)], f32)
    xin[:, NPROMPT:] = xs.reshape(NSAMP * TS, D).T
    st = np.zeros((128, NST), f32)
    sl = slice(NSAMP * b, NSAMP * (b + 1))
    sa = np.asarray(inp["state_conv_a"], f32)[:, sl]
    v = st[:, SA_OFF:SB_OFF].reshape(128, 2, 4, NSEQ, 30)
    v[:, :, :, 1:, :] = sa.reshape(2, NSAMP, 30, 4, 128).transpose(4, 0, 3, 1, 2)
    sb = np.asarray(inp["state_conv_b"], f32)[:, sl]
    v = st[:, SB_OFF:SC_OFF].reshape(128, 2, 4, NSEQ, 2)
    v[:, :, :, 1:, :] = sb.reshape(2, NSAMP, 2, 4, 128).transpose(4, 0, 3, 1, 2)
    sc = np.asarray(inp["state_conv_c"], f32)[:, sl]
    v = st[:, SC_OFF:SH_OFF].reshape(128, 2, 8, NSEQ, 3)
    v[:, :, :, 1:, :] = sc.reshape(2, NSAMP, 3, 8, 128).transpose(4, 0, 3, 1, 2)
    sh = np.asarray(inp["state_lru"], f32)[:, sl]
    v = st[:, SH_OFF:NST].reshape(128, 2, 8, NSEQ)
    v[:, :, :, 1:] = sh.reshape(2, NSAMP, 8, 128).transpose(3, 0, 2, 1)
    return xin, st


_NC_CACHE = {}


def kernel(**inp):
    B = 8
    wt, gw, prm = _prep_shared(inp)
    in_maps = []
    for b in range(B):
        xin, st = _prep_core(inp, b)
        in_maps.append({"xin": xin, "wts": wt, "gws": gw, "prm": prm, "sti": st})
    if "nc" not in _NC_CACHE:
        _NC_CACHE["nc"] = build_program()
    nc = _NC_CACHE["nc"]
    res = run_bass_kernel_spmd(nc, in_maps, core_ids=list(range(B)))
    f32 = np.float32
    y_prompt = np.empty((B, SEQ, D), f32)
    y_sample = np.empty((B * NSAMP, TS, D), f32)
    sa_p = np.empty((2, B, 30, 512), f32)
    sb_p = np.empty((2, B, 2, 512), f32)
    sc_p = np.empty((2, B, 3, 1024), f32)
    sh_p = np.empty((2, B, 1024), f32)
    sa_s = np.empty((2, B * NSAMP, 30, 512), f32)
    sb_s = np.empty((2, B * NSAMP, 2, 512), f32)
    sc_s = np.empty((2, B * NSAMP, 3, 1024), f32)
    sh_s = np.empty((2, B * NSAMP, 1024), f32)
    for b in range(B):
        r = res.results[b]
        yo = np.asarray(r["yout"], f32)
        y_prompt[b] = yo[:, NMETA:NPROMPT].T
        y_sample[NSAMP * b:NSAMP * (b + 1)] = yo[:, NPROMPT:].T.reshape(NSAMP, TS, D)
        st = np.asarray(r["sto"], f32)
        sl = slice(NSAMP * b, NSAMP * (b + 1))
        v = st[:, SA_OFF:SB_OFF].reshape(128, 2, 4, NSEQ, 30)
        t = v.transpose(1, 3, 4, 2, 0).reshape(2, NSEQ, 30, 512)
        sa_p[:, b] = t[:, 0]
        sa_s[:, sl] = t[:, 1:]
        v = st[:, SB_OFF:SC_OFF].reshape(128, 2, 4, NSEQ, 2)
        t = v.transpose(1, 3, 4, 2, 0).reshape(2, NSEQ, 2, 512)
        sb_p[:, b] = t[:, 0]
        sb_s[:, sl] = t[:, 1:]
        v = st[:, SC_OFF:SH_OFF].reshape(128, 2, 8, NSEQ, 3)
        t = v.transpose(1, 3, 4, 2, 0).reshape(2, NSEQ, 3, 1024)
        sc_p[:, b] = t[:, 0]
        sc_s[:, sl] = t[:, 1:]
        v = st[:, SH_OFF:NST].reshape(128, 2, 8, NSEQ)
        t = v.transpose(1, 3, 2, 0).reshape(2, NSEQ, 1024)
        sh_p[:, b] = t[:, 0]
        sh_s[:, sl] = t[:, 1:]
    return (y_prompt, y_sample, sa_p, sb_p, sc_p, sh_p, sa_s, sb_s, sc_s, sh_s)
```

```python
import numpy as np
import concourse.bass as bass
import concourse.mybir as mybir
from concourse.bass_utils import run_bass_kernel_spmd

F32 = mybir.dt.float32
F32R = mybir.dt.float32r
AF = mybir.ActivationFunctionType
ALU = mybir.AluOpType

D = 1024
DEPTH = 4
SEQ = 8192
NMETA = 16
NPROMPT = SEQ + NMETA
NSAMP = 4
TS = 16
NTOK = NPROMPT + NSAMP * TS
CH = 486
NFULL = 16
LASTP = NPROMPT - NFULL * CH
ALPHA = (2 * DEPTH) ** 0.25
LN_EPS = 1e-5
TP = 512
R_SLOTS = 3
K0A = 9
NDA = (4 * (31 - K0A) + 31) // 32
NSEQ = 5

EVEN_IN_ORDER = [1, 0, 4, 3, 2]
ODD_IN_ORDER = [2, 3, 0, 1]


def tile_plan():
    plan = []
    for l in range(DEPTH):
        if l % 2 == 0:
            plan.append(("w_in_e", l // 2, 1))
            plan.append(("w_in_e", l // 2, 0))
            for t in range(NDA):
                plan.append(("diag_a", l // 2, t))
            plan.append(("w_in_e", l // 2, 4))
            plan.append(("w_in_e", l // 2, 3))
            plan.append(("w_in_e", l // 2, 2))
        else:
            plan.append(("w_in_o", l // 2, 2))
            plan.append(("w_in_o", l // 2, 3))
            plan.append(("diag_c", l // 2, 0))
            plan.append(("w_in_o", l // 2, 0))
            plan.append(("w_in_o", l // 2, 1))
        for g in range(2):
            plan.append(("w_out_e" if l % 2 == 0 else "w_out_o", l // 2, g))
        for g in range(8):
            plan.append(("w_mlp1", l, g))
        for g in range(8):
            plan.append(("w_mlp2", l, g))
    return plan


PLAN = tile_plan()
NTILES = len(PLAN)

PCOL = {}
_pc = 0


def _padd(name, n):
    global _pc
    PCOL[name] = _pc
    _pc += n


for _l in range(DEPTH):
    for _nm in ("ln1_g", "ln1_b", "ln2_g", "ln2_b"):
        _padd((_nm, _l), 8)
for _e in range(2):
    _padd(("b_in_e", _e), 20)
    _padd(("conv_a_w", _e), 124)
    _padd(("conv_a_b", _e), 4)
    _padd(("ln_a_g", _e), 4)
    _padd(("ln_a_b", _e), 4)
    _padd(("conv_b_w", _e), 12)
    _padd(("b_out_e", _e), 8)
for _o in range(2):
    _padd(("b_in_o", _o), 16)
    _padd(("conv_c_w", _o), 32)
    _padd(("conv_c_b", _o), 8)
    _padd(("b_gate_a", _o), 8)
    _padd(("b_gate_x", _o), 8)
    _padd(("lru_lambda", _o), 8)
    _padd(("b_out_o", _o), 8)
NPRM = _pc

SA_OFF = 0
SB_OFF = SA_OFF + 2 * 4 * NSEQ * 30
SC_OFF = SB_OFF + 2 * 4 * NSEQ * 2
SH_OFF = SC_OFF + 2 * 8 * NSEQ * 3
NST = SH_OFF + 2 * 8 * NSEQ

_off = 0


def _alloc(n):
    global _off
    o = _off
    _off += n
    return o


X_OFF = _alloc(8 * TP)
X1_OFF = _alloc(8 * TP)
AR_OFF = _alloc(20480)
YO_OFF = AR_OFF + 16384
SQ_OFF = _alloc(4 * TP)
MSQ_OFF = _alloc(TP)
RSTD_OFF = _alloc(TP)
TN_OFF = _alloc(3 * TP)
W_OFF = _alloc(R_SLOTS * 4096)
GW_OFF = _alloc(4096)
PRM_OFF = _alloc(NPRM)
ST_OFF = _alloc(NST)
ONES_OFF = _alloc(256)
DER_OFF = _alloc(96)
CST_OFF = _alloc(4)
S_TOTAL = _off

WU = 30 + LASTP + NSAMP * (30 + TS)
WV = 2 + LASTP + NSAMP * (2 + TS)
WC = 3 + LASTP + NSAMP * (3 + TS)
E_UBUF = AR_OFF
E_VBUF = E_UBUF + 4 * WU
E_GB = E_VBUF + 4 * WV
E_SG = E_GB + 4 * TP
E_HB = E_SG + 4 * TP
E_ACC = E_HB + 4 * TP
E_CA = E_ACC + 4 * TP
E_Y = E_CA + 4 * TP
E_ZT = E_Y + 8 * TP
assert E_ZT + 2 * TP <= AR_OFF + 20480
O_CBUF = AR_OFF
O_XC = AR_OFF + 4096
O_RA = AR_OFF + 8192
O_IB = AR_OFF + 12288
O_A2M = AR_OFF + 16384
M_HID = AR_OFF
M_ZT = AR_OFF + 16384

PS_MAIN = [0, 1, 2, 3, 6, 7]
PS_MEAN = 4
PS_EZ2 = 5


class View:
    __slots__ = ("ap", "keys")

    def __init__(self, ap, keys):
        self.ap = ap
        self.keys = keys


GRAN = 128


def _sb_keys(lo, hi):
    return list(range(lo // GRAN, (hi - 1) // GRAN + 1))


class Prog:
    def __init__(self, nc, S, psum, sems):
        self.nc = nc
        self.S = S
        self.psum = psum
        self.sems = sems
        self.count = {k: 0 for k in sems}
        self.lists = {"pe": [], "act": [], "dve": [], "pool": [], "sp": []}
        self.lastw = {}
        self.readers = {}
        self.waited = {k: {} for k in self.lists}
        self.mult = {k: 1 for k in sems}
        self.targets = {}

    def sb(self, off, n, r=False):
        return View(None, _sb_keys(off, off + n))

    def sb3(self, off, nch, width, c0, c1, r=False, j0=0, j1=None):
        if j1 is None:
            j1 = nch
        return View(None, _sb_keys(off + j0 * width, off + j1 * width))

    def ps(self, bank, c0=0, c1=TP):
        return View(self.psum[bank][:, c0:c1], [100000 + bank])

    def _deps(self, reads, writes):
        deps = {}

        def add(k, c):
            if deps.get(k, 0) < c:
                deps[k] = c

        for v in reads:
            for g in v.keys:
                w = self.lastw.get(g)
                if w:
                    add(*w)
        for v in writes:
            for g in v.keys:
                w = self.lastw.get(g)
                if w:
                    add(*w)
                rd = self.readers.get(g)
                if rd:
                    for k, c in rd.items():
                        add(k, c)
        return deps

    def _emit_waits(self, ek, deps, selfkey):
        lst = self.lists[ek]
        for k, c in deps.items():
            if k == selfkey and ek == "pe":
                continue
            if self.waited[ek].get(k, 0) >= c:
                continue
            self.waited[ek][k] = c
            self.targets.setdefault(k, set()).add(c)
            lst.append(("wait", k, c * self.mult[k]))

    def _record(self, key, cnt, reads, writes):
        for v in writes:
            for g in v.keys:
                self.lastw[g] = (key, cnt)
                self.readers[g] = {}
        for v in reads:
            for g in v.keys:
                d = self.readers.setdefault(g, {})
                if d.get(key, 0) < cnt:
                    d[key] = cnt

    def op(self, ek, fn, reads=(), writes=(), signal=True):
        deps = self._deps(reads, writes)
        self._emit_waits(ek, deps, ek)
        if signal:
            self.count[ek] += 1
            cnt = self.count[ek]
            self.lists[ek].append(("op", fn, ek, 1))
        else:
            cnt = self.count[ek] + 1
            self.lists[ek].append(("op", fn, None, 0))
        self._record(ek, cnt, reads, writes)

    def dma(self, qe, semname, fn, reads=(), writes=()):
        deps = self._deps(reads, writes)
        self._emit_waits(qe, deps, None)
        self.count[semname] += 1
        cnt = self.count[semname]
        self.lists[qe].append(("op", fn, semname, 16))
        self._record(semname, cnt, reads, writes)

    def wait_all(self, ek, semname):
        self.lists[ek].append(("wait", semname, self.count[semname] * self.mult[semname]))

    def replay(self, ek, eng):
        pending = {}
        done = {}
        for it in self.lists[ek]:
            if it[0] == "wait":
                eng.wait_ge(self.sems[it[1]], it[2])
                continue
            ins = it[1](eng)
            k = it[2]
            if k is None:
                continue
            if it[3] != 1:
                ins.then_inc(self.sems[k], it[3])
                continue
            done[k] = done.get(k, 0) + 1
            pending[k] = pending.get(k, 0) + 1
            if done[k] in self.targets.get(k, ()) or pending[k] >= 15:
                ins.then_inc(self.sems[k], pending[k])
                pending[k] = 0


def chunk_groups(c):
    if c < NFULL:
        return [([0], 0, CH, 1)], CH
    return [([0], 0, LASTP, 1), ([1, 2, 3, 4], LASTP, TS, NSAMP)], LASTP + NSAMP * TS


def build_program():
    nc = bass.Bass("TRN2", target_bir_lowering=False)
    xin = nc.dram_tensor("xin", [D, NTOK], F32, kind="ExternalInput").ap()
    wts = nc.dram_tensor("wts", [NTILES, 128, 4096], F32, kind="ExternalInput").ap()
    gws = nc.dram_tensor("gws", [128, 4096], F32, kind="ExternalInput").ap()
    prm = nc.dram_tensor("prm", [128, NPRM], F32, kind="ExternalInput").ap()
    sti = nc.dram_tensor("sti", [128, NST], F32, kind="ExternalInput").ap()
    yout = nc.dram_tensor("yout", [D, NTOK], F32, kind="ExternalOutput").ap()
    sto = nc.dram_tensor("sto", [128, NST], F32, kind="ExternalOutput").ap()

    from contextlib import ExitStack
    with ExitStack() as es:
        slab = es.enter_context(nc.sbuf_tensor("slab", [128, S_TOTAL], F32))
        sbase = nc.lookup_mloc(slab).addr

        class _SProxy:
            WSTRIDE = 2048
            WSIZE = 8192

            def __init__(self):
                self.win = {}

            def __getitem__(self, key):
                _, sl = key
                a, b = sl.start, sl.stop
                st = (a // self.WSTRIDE) * self.WSTRIDE
                size = min(self.WSIZE, S_TOTAL - st)
                assert b <= st + size, (a, b)
                h = self.win.get(st)
                if h is None:
                    h = nc.alloc_sbuf_tensor_at(f"F{st}", [128, size], F32, offset=sbase + st * 4)
                    self.win[st] = h
                return h[:, a - st:b - st]

        S = _SProxy()
        RT = {}

        def addR(name, off, n):
            RT[name] = (off, n, nc.alloc_sbuf_tensor_at(name, [128, n], F32R, offset=sbase + off * 4))

        addR("X", X_OFF, 8 * TP)
        addR("X1", X1_OFF, 8 * TP)
        addR("W", W_OFF, R_SLOTS * 4096)
        addR("GW", GW_OFF, 4096)
        addR("ONES", ONES_OFF, 256)
        addR("SQ", SQ_OFF, 4 * TP)
        addR("HID", M_HID, 32 * TP)
        addR("ECA", E_CA, 4 * TP)
        addR("EY", E_Y, 8 * TP)
        addR("OXC", O_XC, 8 * TP)
        addR("OYR", O_A2M, 8 * TP)
        addR("EUB", E_UBUF, 4 * WU)
        addR("OCB", O_CBUF, 8 * WC)

        def rap(name, a, b):
            off, n, h = RT[name]
            assert off <= a and b <= off + n, (name, a, b)
            return h[:, a - off:b - off]

        RNAME = {X_OFF: "X", X1_OFF: "X1", E_Y: "EY", O_A2M: "OYR", E_CA: "ECA", O_XC: "OXC", M_HID: "HID", E_UBUF: "EUB", O_CBUF: "OCB"}
        psum = [es.enter_context(nc.psum_tensor(f"ps{i}", [128, TP], F32)) for i in range(8)]
        semnames = ["pe", "act", "dve", "pool", "x", "x1", "y", "p0", "p1", "p2", "so"] + [f"w{i}" for i in range(R_SLOTS)]
        sems = {n: es.enter_context(nc.semaphore(n)) for n in semnames}
        P = Prog(nc, S, psum, sems)
        for n in ["x", "x1", "y", "p0", "p1", "p2", "so"] + [f"w{i}" for i in range(R_SLOTS)]:
            P.mult[n] = 16
        block = es.enter_context(nc.Block())

        def pcol(name, j=0):
            o = PRM_OFF + PCOL[name] + j
            return S[:, o:o + 1]

        prm_v = P.sb(PRM_OFF, NPRM)
        st_v = P.sb(ST_OFF, NST)
        der_v = P.sb(DER_OFF, 96)
        cst_v = P.sb(CST_OFF, 4)
        ones_v = P.sb(ONES_OFF, 256, r=True)
        gw_v = P.sb(GW_OFF, 4096, r=True)
        ONE_AP = S[:, CST_OFF:CST_OFF + 1]
        EPS_AP = S[:, CST_OFF + 1:CST_OFF + 2]
        ZERO_AP = S[:, CST_OFF + 2:CST_OFF + 3]

        P.dma("sp", "p0", lambda e: e.dma_start(out=S[:, PRM_OFF:PRM_OFF + NPRM], in_=prm), writes=[prm_v])
        P.dma("sp", "p1", lambda e: e.dma_start(out=S[:, ST_OFF:ST_OFF + NST], in_=sti), writes=[st_v])
        tn0_v = P.sb(TN_OFF, 256)
        P.op("dve", lambda e: e.memset(S[:, TN_OFF:TN_OFF + 128], 1.0 / 1024.0), writes=[tn0_v])
        P.op("dve", lambda e: e.memset(S[:, TN_OFF + 128:TN_OFF + 256], 1.0 / 512.0), writes=[tn0_v])
        P.op("dve", lambda e: e.tensor_copy(out=rap("ONES", ONES_OFF, ONES_OFF + 256), in_=S[:, TN_OFF:TN_OFF + 256]),
             reads=[tn0_v], writes=[ones_v])
        P.op("dve", lambda e: e.memset(S[:, CST_OFF:CST_OFF + 1], 1.0), writes=[cst_v])
        P.op("dve", lambda e: e.memset(S[:, CST_OFF + 1:CST_OFF + 2], LN_EPS), writes=[cst_v])
        P.op("dve", lambda e: e.memset(S[:, CST_OFF + 2:CST_OFF + 4], 0.0), writes=[cst_v])
        P.op("dve", lambda e: e.memset(S[:, DER_OFF + 80:DER_OFF + 81], 0.25), writes=[der_v])
        for o in range(2):
            lam = S[:, PRM_OFF + PCOL[("lru_lambda", o)]:PRM_OFF + PCOL[("lru_lambda", o)] + 8]
            tmp = S[:, DER_OFF + 32 + o * 8:DER_OFF + 40 + o * 8]
            cc = S[:, DER_OFF + o * 8:DER_OFF + o * 8 + 8]
            c2 = S[:, DER_OFF + 16 + o * 8:DER_OFF + 24 + o * 8]
            P.op("act", lambda e, lam=lam, tmp=tmp: e.activation(out=tmp, in_=lam, func=AF.Exp, scale=-1.0),
                 reads=[prm_v], writes=[der_v])
            P.op("act", lambda e, tmp=tmp: e.activation(out=tmp, in_=tmp, func=AF.Ln, bias=ONE_AP, scale=1.0),
                 reads=[der_v, cst_v], writes=[der_v])
            P.op("dve", lambda e, tmp=tmp, cc=cc: e.tensor_scalar(out=cc, in0=tmp, scalar1=-8.0, scalar2=None, op0=ALU.mult),
                 reads=[der_v], writes=[der_v])
            P.op("dve", lambda e, tmp=tmp, c2=c2: e.tensor_scalar(out=c2, in0=tmp, scalar1=-4.0, scalar2=None, op0=ALU.mult),
                 reads=[der_v], writes=[der_v])
            for nm, do in (("b_gate_a", 48), ("b_gate_x", 64)):
                src = S[:, PRM_OFF + PCOL[(nm, o)]:PRM_OFF + PCOL[(nm, o)] + 8]
                dst = S[:, DER_OFF + do + o * 8:DER_OFF + do + o * 8 + 8]
                P.op("dve", lambda e, src=src, dst=dst: e.tensor_scalar(out=dst, in0=src, scalar1=0.5, scalar2=None, op0=ALU.mult),
                     reads=[prm_v], writes=[der_v])

        wstate = {"issued": 0, "total": 0}

        def w_slot_view(slot):
            return P.sb(W_OFF + slot * 4096, 4096, r=True)

        def pump(limit):
            while wstate["issued"] < min(limit, wstate["total"]):
                g = wstate["issued"]
                slot = g % R_SLOTS
                t = g % NTILES
                dst = rap("W", W_OFF + slot * 4096, W_OFF + (slot + 1) * 4096)
                P.dma("pool", f"w{slot}", lambda e, dst=dst, t=t: e.dma_start(out=dst, in_=wts[t]),
                      writes=[w_slot_view(slot)])
                wstate["issued"] += 1

        nchunks = NFULL + 1
        wstate["total"] = nchunks * NTILES
        wcur = {"g": 0}

        def next_tile():
            g = wcur["g"]
            pump(g + 1)
            slot = g % R_SLOTS
            wcur["g"] += 1
            return slot

        def tile_done():
            pump(wcur["g"] + R_SLOTS - 1 + 1)

        psrot = {"i": 0}

        def next_bank():
            b = PS_MAIN[psrot["i"] % len(PS_MAIN)]
            psrot["i"] += 1
            return b

        def wl(slot, kc, q, kind):
            base = W_OFF + slot * 4096
            if kind == "in":
                o = base + kc * 512 + q * 128
            else:
                o = base + kc * 128
            return rap("W", o, o + 128)

        def mm_group(bank, T, pairs, reads):
            n = len(pairs)
            out = psum[bank][:, 0:T]
            for i, (l, r) in enumerate(pairs):
                P.op("pe", lambda e, l=l, r=r, i=i: e.matmul(out, l, r, start=(i == 0), stop=(i == n - 1)),
                     reads=reads, writes=[P.ps(bank)], signal=(i == n - 1))

        curx = {"off": X_OFF}

        def x_chunk(kc, T, r=True, off=None):
            if off is None:
                off = curx["off"]
            return P.sb3(off, 8, TP, 0, T, r=r, j0=kc, j1=kc + 1)

        def x_ap(kc, T, r=True, off=None):
            if off is None:
                off = curx["off"]
            a = off + kc * TP
            return rap(RNAME[off], a, a + T) if r else S[:, a:a + T]

        def buf_ap(off, j, width, c0, c1, r=False):
            a = off + j * width
            return rap(RNAME[off], a + c0, a + c1) if r else S[:, a + c0:a + c1]

        def buf_v(off, j, width, r=False):
            return P.sb(off + j * width, width, r=r)

        def grp_dense(ap2d_fn, grp):
            seqs, col0, n, count = grp
            ap = ap2d_fn(col0, col0 + count * n)
            if count > 1:
                ap = ap.rearrange("p (s c) -> p s c", c=n)
            return ap

        def pad_start(groups, gi, H):
            st = 0
            for g in groups[:gi]:
                st += g[3] * (H + g[2])
            return st

        def grp_padded(ap2d_fn, groups, gi, H, shift):
            seqs, col0, n, count = groups[gi]
            ps0 = pad_start(groups, gi, H)
            if count == 1:
                return ap2d_fn(ps0 + shift, ps0 + shift + n)
            ap = ap2d_fn(ps0, ps0 + count * (H + n)).rearrange("p (s c) -> p s c", c=H + n)
            return ap[:, :, shift:shift + n]

        def act_preload_ln():
            dv = P.sb(CST_OFF + 3, 1)
            P.op("act", lambda e: e.activation(out=S[:, CST_OFF + 3:CST_OFF + 4], in_=ONE_AP, func=AF.Ln),
                 reads=[cst_v], writes=[dv])

        def ln_sq(T, j, nslots, get_ap, get_v):
            so = SQ_OFF + (j % nslots) * TP
            sqr = rap("SQ", so, so + T)
            zin = get_ap(j, False)
            P.op("act", lambda e, sqr=sqr, zin=zin: e.activation(out=sqr, in_=zin, func=AF.Square),
                 reads=[get_v(j)], writes=[P.sb(so, TP)])

        def ln_mm(T, j, nch, nslots, get_ap, get_v, ones_off):
            ones_ap = rap("ONES", ones_off, ones_off + 128)
            so = SQ_OFF + (j % nslots) * TP
            sqr = rap("SQ", so, so + T)
            zr = get_ap(j, True)
            P.op("pe", lambda e, zr=zr, j=j: e.matmul(psum[PS_MEAN][:, 0:T], ones_ap, zr, start=(j == 0), stop=(j == nch - 1)),
                 reads=[ones_v, get_v(j)], writes=[P.ps(PS_MEAN)], signal=True)
            P.op("pe", lambda e, sqr=sqr, j=j: e.matmul(psum[PS_EZ2][:, 0:T], ones_ap, sqr, start=(j == 0), stop=(j == nch - 1)),
                 reads=[ones_v, P.sb(so, TP)], writes=[P.ps(PS_EZ2)], signal=True)

        def ln_finish(T, nch, get_ap, get_v, g_name, b_name, func, out_ap, out_v):
            msq = S[:, MSQ_OFF:MSQ_OFF + T]
            msqv = P.sb(MSQ_OFF, TP)
            rstd = S[:, RSTD_OFF:RSTD_OFF + T]
            rstdv = P.sb(RSTD_OFF, TP)
            P.op("act", lambda e: e.activation(out=msq, in_=psum[PS_MEAN][:, 0:T], func=AF.Square),
                 reads=[P.ps(PS_MEAN)], writes=[msqv])
            P.op("dve", lambda e: e.tensor_tensor(out=rstd, in0=psum[PS_EZ2][:, 0:T], in1=msq, op=ALU.subtract),
                 reads=[P.ps(PS_EZ2), msqv], writes=[rstdv])
            P.op("act", lambda e: e.activation(out=rstd, in_=rstd, func=AF.Ln, bias=EPS_AP, scale=1.0),
                 reads=[rstdv, cst_v], writes=[rstdv])
            P.op("act", lambda e: e.activation(out=rstd, in_=rstd, func=AF.Exp, scale=-0.5),
                 reads=[rstdv], writes=[rstdv])
            nearly = min(3, nch)

            def emit_sub(j):
                zj = get_ap(j, False)
                to = TN_OFF + (j % 3) * TP
                tn = S[:, to:to + T]
                P.op("dve", lambda e, zj=zj, tn=tn: e.tensor_tensor(out=tn, in0=zj, in1=psum[PS_MEAN][:, 0:T], op=ALU.subtract),
                     reads=[get_v(j), P.ps(PS_MEAN)], writes=[P.sb(to, TP)])

            def emit_rest(j):
                to = TN_OFF + (j % 3) * TP
                tn = S[:, to:to + T]
                tnv = P.sb(to, TP)
                P.op("dve", lambda e, tn=tn: e.tensor_tensor(out=tn, in0=tn, in1=rstd, op=ALU.mult),
                     reads=[tnv, rstdv], writes=[tnv])
                oj = out_ap(j)
                P.op("act", lambda e, tn=tn, j=j, oj=oj: e.activation(out=oj, in_=tn, func=func,
                                                                      scale=pcol(g_name, j), bias=pcol(b_name, j)),
                     reads=[tnv, prm_v], writes=[out_v(j)])

            for j in range(nearly):
                emit_sub(j)
            for j in range(nch):
                if j >= nearly:
                    emit_sub(j)
                emit_rest(j)

        def hist_in(off, nch, width, groups, H, st_off, idx, r=False):
            for gi, (seqs, col0, n, count) in enumerate(groups):
                ps0 = pad_start(groups, gi, H)
                for si, s in enumerate(seqs):
                    c0 = ps0 + si * (H + n)
                    base = rap(RNAME[off], off, off + nch * width) if r else S[:, off:off + nch * width]
                    dst = base.rearrange("p (j c) -> p j c", c=width)[:, :, c0:c0 + H]
                    sb0 = st_off + idx * nch * NSEQ * H
                    src = S[:, sb0:sb0 + nch * NSEQ * H].rearrange("p (j s h) -> p j s h", s=NSEQ, h=H)[:, :, s, :]
                    P.op("dve", lambda e, dst=dst, src=src: e.tensor_copy(out=dst, in_=src),
                         reads=[st_v], writes=[P.sb(off, nch * width)])

        def hist_out(off, nch, width, groups, H, st_off, idx):
            for gi, (seqs, col0, n, count) in enumerate(groups):
                ps0 = pad_start(groups, gi, H)
                for si, s in enumerate(seqs):
                    c0 = ps0 + si * (H + n) + n
                    src = S[:, off:off + nch * width].rearrange("p (j c) -> p j c", c=width)[:, :, c0:c0 + H]
                    sb0 = st_off + idx * nch * NSEQ * H
                    dst = S[:, sb0:sb0 + nch * NSEQ * H].rearrange("p (j s h) -> p j s h", s=NSEQ, h=H)[:, :, s, :]
                    P.op("dve", lambda e, dst=dst, src=src: e.tensor_copy(out=dst, in_=src),
                         reads=[P.sb(off, nch * width)], writes=[st_v])

        def dwconv(in_off, nch, width, groups, H, ntaps, w_name, bias_name, acc_off, fin_off, fin_r, only=None, js=None):
            for k in range(ntaps):
                last = (k == ntaps - 1)
                for j in (js if js is not None else range(nch)):
                    for gi, grp in enumerate(groups):
                        if only is not None and gi not in only:
                            continue
                        src = grp_padded(lambda a, b, j=j: buf_ap(in_off, j, width, a, b), groups, gi, H, k)
                        if last:
                            dst = grp_dense(lambda a, b, j=j: buf_ap(fin_off, j, TP, a, b, fin_r), grp)
                            dstv = buf_v(fin_off, j, TP)
                        else:
                            dst = grp_dense(lambda a, b, j=j: buf_ap(acc_off, j, TP, a, b), grp)
                            dstv = buf_v(acc_off, j, TP)
                        wk = pcol(w_name, j * ntaps + k)
                        if k == 0:
                            if bias_name is not None:
                                P.op("dve", lambda e, dst=dst, src=src, wk=wk, j=j: e.tensor_scalar(
                                    out=dst, in0=src, scalar1=wk, scalar2=pcol(bias_name, j), op0=ALU.mult, op1=ALU.add),
                                    reads=[buf_v(in_off, j, width), prm_v], writes=[dstv])
                            else:
                                P.op("dve", lambda e, dst=dst, src=src, wk=wk: e.tensor_scalar(
                                    out=dst, in0=src, scalar1=wk, scalar2=None, op0=ALU.mult),
                                    reads=[buf_v(in_off, j, width), prm_v], writes=[dstv])
                        else:
                            acc = grp_dense(lambda a, b, j=j: buf_ap(acc_off, j, TP, a, b), grp)
                            P.op("dve", lambda e, dst=dst, src=src, wk=wk, acc=acc: e.scalar_tensor_tensor(
                                out=dst, in0=src, scalar=wk, in1=acc, op0=ALU.mult, op1=ALU.add),
                                reads=[buf_v(in_off, j, width), prm_v, buf_v(acc_off, j, TP)], writes=[dstv])

        def tile_kouter(slot, T, rhs_ap, rhs_v, evac):
            banks = [next_bank() for _ in range(4)]
            for kc in range(8):
                for q in range(4):
                    l_ = wl(slot, kc, q, "in")
                    r_ = rhs_ap(kc)
                    P.op("pe", lambda e, kc=kc, q=q, l_=l_, r_=r_: e.matmul(psum[banks[q]][:, 0:T], l_, r_,
                                                                            start=(kc == 0), stop=(kc == 7)),
                         reads=[w_slot_view(slot), rhs_v(kc)], writes=[P.ps(banks[q])], signal=(kc == 7))
            for q in range(4):
                evac(q, banks[q])

        def pe_diag_conv(nch, ntaps, in_off, width, H, n0, w_name, bias_name, fin_off, k0=0, acc_off=None, evac="act"):
            if k0 > 0:
                for k in range(k0):
                    for j in range(nch):
                        src = buf_ap(in_off, j, width, k, k + n0)
                        acc = buf_ap(acc_off, j, TP, 0, n0)
                        wk = pcol(w_name, j * ntaps + k)
                        if k == 0:
                            P.op("dve", lambda e, acc=acc, src=src, wk=wk, j=j: e.tensor_scalar(
                                out=acc, in0=src, scalar1=wk, scalar2=pcol(bias_name, j), op0=ALU.mult, op1=ALU.add),
                                reads=[buf_v(in_off, j, width), prm_v], writes=[buf_v(acc_off, j, TP)])
                        else:
                            P.op("dve", lambda e, acc=acc, src=src, wk=wk: e.scalar_tensor_tensor(
                                out=acc, in0=src, scalar=wk, in1=acc, op0=ALU.mult, op1=ALU.add),
                                reads=[buf_v(in_off, j, width), prm_v, buf_v(acc_off, j, TP)], writes=[buf_v(acc_off, j, TP)])
            ntp = ntaps - k0
            nmat = nch * ntp
            slot = None
            for m in range(nmat):
                if m % 32 == 0:
                    if slot is not None:
                        tile_done()
                    slot = next_tile()
                j, kk = divmod(m, ntp)
                k = k0 + kk
                if kk == 0:
                    bank = next_bank()
                o = W_OFF + slot * 4096 + (m % 32) * 128
                lhs = rap("W", o, o + 128)
                rhs = buf_ap(in_off, j, width, k, k + n0, True)
                P.op("pe", lambda e, lhs=lhs, rhs=rhs, bank=bank, kk=kk: e.matmul(psum[bank][:, 0:n0], lhs, rhs, start=(kk == 0), stop=(kk == ntp - 1)),
                     reads=[w_slot_view(slot), buf_v(in_off, j, width)], writes=[P.ps(bank)], signal=(kk == ntp - 1))
                if kk == ntp - 1:
                    dst = buf_ap(fin_off, j, TP, 0, n0, True)
                    if k0 > 0:
                        acc = buf_ap(acc_off, j, TP, 0, n0)
                        P.op("dve", lambda e, dst=dst, bank=bank, acc=acc: e.tensor_tensor(out=dst, in0=psum[bank][:, 0:n0], in1=acc, op=ALU.add),
                             reads=[P.ps(bank), buf_v(acc_off, j, TP)], writes=[buf_v(fin_off, j, TP)])
                    elif evac == "dve":
                        P.op("dve", lambda e, dst=dst, bank=bank, j=j: e.tensor_scalar(out=dst, in0=psum[bank][:, 0:n0], scalar1=pcol(bias_name, j),
                                                                                      scalar2=None, op0=ALU.add),
                             reads=[P.ps(bank), prm_v], writes=[buf_v(fin_off, j, TP)])
                    else:
                        P.op("act", lambda e, dst=dst, bank=bank, j=j: e.activation(out=dst, in_=psum[bank][:, 0:n0], func=AF.Identity,
                                                                                   bias=pcol(bias_name, j), scale=1.0),
                             reads=[P.ps(bank), prm_v], writes=[buf_v(fin_off, j, TP)])
            tile_done()

        def x_get(T):
            return (lambda j, r: x_ap(j, T, r)), (lambda j: x_chunk(j, T))

        def out_proj_and_ln1(l, T, y_off, zt_off, bname):
            xg, xv = x_get(T)
            lnmm = {"n": 0}
            for g in range(2):
                slot = next_tile()
                if g == 0:
                    g0banks = []
                    tile_kouter(slot, T, lambda kc: x_ap(kc, T, True, y_off), lambda kc: x_chunk(kc, T, True, y_off),
                                lambda q, bank: g0banks.append(bank))
                for q in range(4):
                    oc = g * 4 + q
                    if g == 0:
                        bank = g0banks[q]
                    else:
                        bank = next_bank()
                        mm_group(bank, T, [(wl(slot, kc, q, "in"), x_ap(kc, T, True, y_off)) for kc in range(8)],
                                 reads=[w_slot_view(slot)] + [x_chunk(kc, T, True, y_off) for kc in range(8)])
                    if g == 1:
                        while lnmm["n"] <= oc - 2:
                            ln_mm(T, lnmm["n"], 8, 4, xg, xv, ONES_OFF)
                            lnmm["n"] += 1
                    zt = S[:, zt_off + (oc % 2) * TP:zt_off + (oc % 2) * TP + T]
                    ztv = P.sb(zt_off + (oc % 2) * TP, TP)
                    P.op("act", lambda e, zt=zt, bank=bank, oc=oc: e.activation(out=zt, in_=psum[bank][:, 0:T], func=AF.Identity,
                                                                               bias=pcol(bname, oc), scale=1.0),
                         reads=[P.ps(bank), prm_v], writes=[ztv])
                    if oc >= 1:
                        ln_sq(T, oc - 1, 4, xg, xv)
                    xo = x_ap(oc, T, True)
                    xf = x_ap(oc, T, False)
                    P.op("dve", lambda e, zt=zt, xo=xo, xf=xf: e.scalar_tensor_tensor(out=xo, in0=xf, scalar=ALPHA, in1=zt,
                                                                              op0=ALU.mult, op1=ALU.add),
                         reads=[ztv, x_chunk(oc, T)], writes=[x_chunk(oc, T)])
                tile_done()
            ln_sq(T, 7, 4, xg, xv)
            ln_mm(T, 6, 8, 4, xg, xv, ONES_OFF)
            ln_mm(T, 7, 8, 4, xg, xv, ONES_OFF)
            ln_finish(T, 8, xg, xv, ("ln1_g", l), ("ln1_b", l), AF.Identity,
                      lambda j: x_ap(j, T, True), lambda j: x_chunk(j, T))

        def mlp_and_ln2(l, T, last):
            for g in range(8):
                slot = next_tile()
                if g == 0:
                    g0banks = []
                    tile_kouter(slot, T, lambda kc: x_ap(kc, T), lambda kc: x_chunk(kc, T), lambda q, bank: g0banks.append(bank))
                for q in range(4):
                    f = g * 4 + q
                    if g == 0:
                        bank = g0banks[q]
                    else:
                        bank = next_bank()
                        mm_group(bank, T, [(wl(slot, kc, q, "in"), x_ap(kc, T)) for kc in range(8)],
                                 reads=[w_slot_view(slot)] + [x_chunk(kc, T) for kc in range(8)])
                    zt = S[:, M_ZT + (f % 2) * TP:M_ZT + (f % 2) * TP + T]
                    ztv = P.sb(M_ZT + (f % 2) * TP, TP)
                    P.op("act", lambda e, zt=zt, bank=bank: e.activation(out=zt, in_=psum[bank][:, 0:T], func=AF.Relu),
                         reads=[P.ps(bank)], writes=[ztv])
                    hf = rap("HID", M_HID + f * TP, M_HID + f * TP + T)
                    P.op("dve", lambda e, zt=zt, hf=hf: e.tensor_tensor(out=hf, in0=zt, in1=zt, op=ALU.mult),
                         reads=[ztv], writes=[P.sb(M_HID + f * TP, TP)])
                tile_done()
            xg, xv = x_get(T)
            for oc in range(8):
                slot = next_tile()
                bank = next_bank()
                mm_group(bank, T, [(wl(slot, kc, 0, "m2"), rap("HID", M_HID + kc * TP, M_HID + kc * TP + T)) for kc in range(32)],
                         reads=[w_slot_view(slot), P.sb(M_HID, 32 * TP)])
                if oc >= 1:
                    ln_mm(T, oc - 1, 8, 4, xg, xv, ONES_OFF)
                xo = x_ap(oc, T, True)
                xf = x_ap(oc, T, False)
                P.op("dve", lambda e, xo=xo, xf=xf, bank=bank: e.scalar_tensor_tensor(out=xo, in0=xf, scalar=ALPHA,
                                                                              in1=psum[bank][:, 0:T], op0=ALU.mult, op1=ALU.add),
                     reads=[P.ps(bank), x_chunk(oc, T)], writes=[x_chunk(oc, T)])
                ln_sq(T, oc, 4, xg, xv)
                tile_done()
            ln_mm(T, 7, 8, 4, xg, xv, ONES_OFF)
            if last:
                ln_finish(T, 8, xg, xv, ("ln2_g", l), ("ln2_b", l), AF.Identity,
                          lambda j: x_ap(j, T, False, YO_OFF), lambda j: x_chunk(j, T, False, YO_OFF))
            else:
                ln_finish(T, 8, xg, xv, ("ln2_g", l), ("ln2_b", l), AF.Identity,
                          lambda j: x_ap(j, T, True), lambda j: x_chunk(j, T))

        def even_layer(l, T, groups):
            e_ = l // 2
            nA = groups[0][2]
            x_reads = [x_chunk(kc, T) for kc in range(8)]
            hist_in(E_UBUF, 4, WU, groups, 30, SA_OFF + ST_OFF, e_, r=True)
            hist_in(E_VBUF, 4, WV, groups, 2, SB_OFF + ST_OFF, e_)

            def in_tile(evac):
                slot = next_tile()
                for q in range(4):
                    bank = next_bank()
                    mm_group(bank, T, [(wl(slot, kc, q, "in"), x_ap(kc, T)) for kc in range(8)],
                             reads=[w_slot_view(slot)] + x_reads)
                    evac(q, bank)
                tile_done()

            def ev_gate(q, bank):
                sg = S[:, E_SG + q * TP:E_SG + q * TP + T]
                P.op("act", lambda e: e.activation(out=sg, in_=psum[bank][:, 0:T], func=AF.Sigmoid,
                                                   bias=pcol(("b_in_e", e_), 4 + q), scale=1.0),
                     reads=[P.ps(bank), prm_v], writes=[P.sb(E_SG + q * TP, TP)])
            slot0 = next_tile()
            tile_kouter(slot0, T, lambda kc: x_ap(kc, T), lambda kc: x_chunk(kc, T), ev_gate)
            tile_done()
            act_preload_ln()

            def ev_val(q, bank):
                for gi, grp in enumerate(groups):
                    dst = grp_padded(lambda a, b: buf_ap(E_UBUF, q, WU, a, b, True), groups, gi, 30, 30)
                    src = grp_dense(lambda a, b: psum[bank][:, a:b], grp)
                    sgg = grp_dense(lambda a, b: S[:, E_SG + q * TP + a:E_SG + q * TP + b], grp)
                    P.op("dve", lambda e, dst=dst, src=src, sgg=sgg: e.scalar_tensor_tensor(
                        out=dst, in0=src, scalar=pcol(("b_in_e", e_), q), in1=sgg, op0=ALU.add, op1=ALU.mult),
                        reads=[P.ps(bank), prm_v, P.sb(E_SG + q * TP, TP)], writes=[buf_v(E_UBUF, q, WU)])
            in_tile(ev_val)

            pe_diag_conv(4, 31, E_UBUF, WU, 30, nA, ("conv_a_w", e_), ("conv_a_b", e_), E_CA, k0=K0A, acc_off=E_ACC)
            if len(groups) > 1:
                dwconv(E_UBUF, 4, WU, groups, 30, 31, ("conv_a_w", e_), ("conv_a_b", e_), E_ACC, E_CA, True, only=[1])
            hist_out(E_UBUF, 4, WU, groups, 30, SA_OFF + ST_OFF, e_)
            cag = lambda j, r: buf_ap(E_CA, j, TP, 0, T, r)
            cav = lambda j: buf_v(E_CA, j, TP)
            for j in range(4):
                ln_sq(T, j, 4, cag, cav)

            def ev_hb(q, bank):
                hb = S[:, E_HB + q * TP:E_HB + q * TP + T]
                P.op("act", lambda e: e.activation(out=hb, in_=psum[bank][:, 0:T], func=AF.Identity,
                                                   bias=pcol(("b_in_e", e_), 16 + q), scale=1.0),
                     reads=[P.ps(bank), prm_v], writes=[P.sb(E_HB + q * TP, TP)])
            in_tile(ev_hb)
            for j in range(4):
                ln_mm(T, j, 4, 4, cag, cav, ONES_OFF + 128)
            ln_finish(T, 4, cag, cav, ("ln_a_g", e_), ("ln_a_b", e_), AF.Silu,
                      lambda j: buf_ap(E_Y, j, TP, 0, T, True), lambda j: buf_v(E_Y, j, TP))

            def ev_gc(q, bank):
                for gi, grp in enumerate(groups):
                    dst = grp_padded(lambda a, b: buf_ap(E_VBUF, q, WV, a, b), groups, gi, 2, 2)
                    src = grp_dense(lambda a, b: psum[bank][:, a:b], grp)
                    hbg = grp_dense(lambda a, b: S[:, E_HB + q * TP + a:E_HB + q * TP + b], grp)
                    P.op("dve", lambda e, dst=dst, src=src, hbg=hbg: e.scalar_tensor_tensor(
                        out=dst, in0=src, scalar=pcol(("b_in_e", e_), 12 + q), in1=hbg, op0=ALU.add, op1=ALU.mult),
                        reads=[P.ps(bank), prm_v, P.sb(E_HB + q * TP, TP)], writes=[buf_v(E_VBUF, q, WV)])
            in_tile(ev_gc)

            def ev_gb(q, bank):
                gb = S[:, E_GB + q * TP:E_GB + q * TP + T]
                P.op("act", lambda e: e.activation(out=gb, in_=psum[bank][:, 0:T], func=AF.Identity,
                                                   bias=pcol(("b_in_e", e_), 8 + q), scale=1.0),
                     reads=[P.ps(bank), prm_v], writes=[P.sb(E_GB + q * TP, TP)])
            in_tile(ev_gb)

            act_preload_ln()
            dwconv(E_VBUF, 4, WV, groups, 2, 3, ("conv_b_w", e_), None, E_ACC, E_ACC, False)
            hist_out(E_VBUF, 4, WV, groups, 2, SB_OFF + ST_OFF, e_)
            for j in range(4):
                yb = buf_ap(E_Y, 4 + j, TP, 0, T, True)
                cb = buf_ap(E_ACC, j, TP, 0, T)
                gb = S[:, E_GB + j * TP:E_GB + j * TP + T]
                P.op("dve", lambda e, yb=yb, gb=gb, cb=cb: e.tensor_tensor(out=yb, in0=cb, in1=gb, op=ALU.mult),
                     reads=[buf_v(E_ACC, j, TP), P.sb(E_GB + j * TP, TP)], writes=[buf_v(E_Y, 4 + j, TP)])
            out_proj_and_ln1(l, T, E_Y, E_ZT, ("b_out_e", e_))

        def odd_layer(l, T, groups):
            o_ = l // 2
            nA = groups[0][2]
            x_reads = [x_chunk(kc, T) for kc in range(8)]
            hist_in(O_CBUF, 8, WC, groups, 3, SC_OFF + ST_OFF, o_, r=True)
            O_GGF = SQ_OFF
            QUART = S[:, DER_OFF + 80:DER_OFF + 81]
            tbase = wcur["g"]
            sU = [next_tile(), next_tile()]
            sC = next_tile()

            def emit_u_evac(h, bank):
                for gi, grp in enumerate(groups):
                    dst = grp_padded(lambda a, b: buf_ap(O_CBUF, h, WC, a, b, True), groups, gi, 3, 3)
                    src = grp_dense(lambda a, b: psum[bank][:, a:b], grp)
                    P.op("act", lambda e, dst=dst, src=src, h=h: e.activation(out=dst, in_=src, func=AF.Identity,
                                                                              bias=pcol(("b_in_o", o_), 8 + h), scale=1.0),
                         reads=[P.ps(bank), prm_v], writes=[buf_v(O_CBUF, h, WC)])

            def emit_u(h):
                bank = next_bank()
                mm_group(bank, T, [(wl(sU[1], kc, h - 4, "in"), x_ap(kc, T)) for kc in range(8)],
                         reads=[w_slot_view(sU[1])] + x_reads)
                emit_u_evac(h, bank)

            def emit_conv(h):
                bank = next_bank()
                for k in range(4):
                    o = W_OFF + sC * 4096 + (h * 4 + k) * 128
                    lhs = rap("W", o, o + 128)
                    rhs = buf_ap(O_CBUF, h, WC, k, k + nA, True)
                    P.op("pe", lambda e, lhs=lhs, rhs=rhs, bank=bank, k=k: e.matmul(psum[bank][:, 0:nA], lhs, rhs, start=(k == 0), stop=(k == 3)),
                         reads=[w_slot_view(sC), buf_v(O_CBUF, h, WC)], writes=[P.ps(bank)], signal=(k == 3))
                dst = buf_ap(O_XC, h, TP, 0, nA, True)
                P.op("dve", lambda e, dst=dst, bank=bank, h=h: e.tensor_scalar(out=dst, in0=psum[bank][:, 0:nA], scalar1=pcol(("conv_c_b", o_), h),
                                                                              scalar2=None, op0=ALU.add),
                     reads=[P.ps(bank), prm_v], writes=[buf_v(O_XC, h, TP)])
                if len(groups) > 1:
                    dwconv(O_CBUF, 8, WC, groups, 3, 4, ("conv_c_w", o_), ("conv_c_b", o_), O_A2M, O_XC, True, only=[1], js=[h])

            def emit_gates(h):
                bank = next_bank()
                go = GW_OFF + ((o_ * 2 + 0) * 8 + h) * 128
                mm_group(bank, T, [(rap("GW", go, go + 128), buf_ap(O_XC, h, TP, 0, T, True))], reads=[gw_v, buf_v(O_XC, h, TP)])
                bank2 = next_bank()
                go2 = GW_OFF + ((o_ * 2 + 1) * 8 + h) * 128
                mm_group(bank2, T, [(rap("GW", go2, go2 + 128), buf_ap(O_XC, h, TP, 0, T, True))], reads=[gw_v, buf_v(O_XC, h, TP)])
                ra = buf_ap(O_RA, h, TP, 0, T)
                a2 = buf_ap(O_A2M, h, TP, 0, T)
                ib = buf_ap(O_IB, h, TP, 0, T)
                xc = buf_ap(O_XC, h, TP, 0, T)
                hba = S[:, DER_OFF + 48 + o_ * 8 + h:DER_OFF + 48 + o_ * 8 + h + 1]
                hbx = S[:, DER_OFF + 64 + o_ * 8 + h:DER_OFF + 64 + o_ * 8 + h + 1]
                hc = S[:, DER_OFF + 16 + o_ * 8 + h:DER_OFF + 16 + o_ * 8 + h + 1]
                P.op("act", lambda e: e.activation(out=ra, in_=psum[bank][:, 0:T], func=AF.Tanh, bias=hba, scale=0.5),
                     reads=[P.ps(bank), der_v], writes=[buf_v(O_RA, h, TP)])
                P.op("act", lambda e: e.activation(out=ib, in_=psum[bank2][:, 0:T], func=AF.Tanh, bias=hbx, scale=0.5),
                     reads=[P.ps(bank2), der_v], writes=[buf_v(O_IB, h, TP)])
                P.op("act", lambda e: e.activation(out=ra, in_=ra, func=AF.Exp, scale=hc, bias=hc),
                     reads=[buf_v(O_RA, h, TP), der_v], writes=[buf_v(O_RA, h, TP)])
                P.op("dve", lambda e: e.scalar_tensor_tensor(out=ib, in0=ib, scalar=1.0, in1=xc, op0=ALU.add, op1=ALU.mult),
                     reads=[buf_v(O_XC, h, TP), buf_v(O_IB, h, TP)], writes=[buf_v(O_IB, h, TP)])
                P.op("dve", lambda e: e.tensor_tensor(out=a2, in0=ra, in1=ra, op=ALU.mult),
                     reads=[buf_v(O_RA, h, TP)], writes=[buf_v(O_A2M, h, TP)])
                P.op("dve", lambda e: e.tensor_scalar(out=a2, in0=a2, scalar1=1.0, scalar2=-1.0, op0=ALU.min, op1=ALU.mult),
                     reads=[buf_v(O_A2M, h, TP)], writes=[buf_v(O_A2M, h, TP)])

            tile_kouter(sU[0], T, lambda kc: x_ap(kc, T), lambda kc: x_chunk(kc, T), emit_u_evac)
            pump(tbase + 4)
            for step in range(9):
                if step < 8:
                    emit_conv(step)
                if step < 4:
                    emit_u(4 + step)
                    if step == 3:
                        pump(tbase + 5)
                        hist_out(O_CBUF, 8, WC, groups, 3, SC_OFF + ST_OFF, o_)
                if step >= 1:
                    emit_gates(step - 1)
            pump(tbase + 6)
            for h in range(8):
                a2 = buf_ap(O_A2M, h, TP, 0, T)
                P.op("act", lambda e, a2=a2: e.activation(out=a2, in_=a2, func=AF.Sqrt, bias=QUART, scale=0.25),
                     reads=[buf_v(O_A2M, h, TP), der_v], writes=[buf_v(O_A2M, h, TP)])
            for g in range(2):
                slot = next_tile()
                for q in range(4):
                    j = g * 4 + q
                    bank = next_bank()
                    mm_group(bank, T, [(wl(slot, kc, q, "in"), x_ap(kc, T)) for kc in range(8)],
                             reads=[w_slot_view(slot)] + x_reads)
                    gg = S[:, O_GGF + j * TP:O_GGF + j * TP + T]
                    P.op("act", lambda e, gg=gg, bank=bank, j=j: e.activation(out=gg, in_=psum[bank][:, 0:T], func=AF.Gelu_apprx_tanh,
                                                                              bias=pcol(("b_in_o", o_), j), scale=1.0),
                         reads=[P.ps(bank), prm_v], writes=[P.sb(O_GGF + j * TP, TP)])
                tile_done()
            act_preload_ln()
            O_HH = O_CBUF
            for h in range(8):
                a2 = buf_ap(O_A2M, h, TP, 0, T)
                ib = buf_ap(O_IB, h, TP, 0, T)
                P.op("dve", lambda e, a2=a2, ib=ib: e.tensor_tensor(out=ib, in0=ib, in1=a2, op=ALU.mult),
                     reads=[buf_v(O_A2M, h, TP), buf_v(O_IB, h, TP)], writes=[buf_v(O_IB, h, TP)])
                for (seqs, col0, n, count) in groups:
                    for si, s in enumerate(seqs):
                        c0 = col0 + si * n
                        sidx = ST_OFF + SH_OFF + (o_ * 8 + h) * NSEQ + s
                        init = S[:, sidx:sidx + 1]
                        P.op("dve", lambda e, h=h, c0=c0, n=n, init=init: e.tensor_tensor_scan(
                            out=buf_ap(O_HH, h, TP, c0, c0 + n), data0=buf_ap(O_RA, h, TP, c0, c0 + n),
                            data1=buf_ap(O_IB, h, TP, c0, c0 + n), initial=init, op0=ALU.mult, op1=ALU.add),
                            reads=[buf_v(O_RA, h, TP), buf_v(O_IB, h, TP), st_v], writes=[buf_v(O_HH, h, TP)])
                gg = S[:, O_GGF + h * TP:O_GGF + h * TP + T]
                hh = buf_ap(O_HH, h, TP, 0, T)
                yr = buf_ap(O_A2M, h, TP, 0, T, True)
                P.op("pool", lambda e, gg=gg, hh=hh, yr=yr: e.tensor_tensor(out=yr, in0=gg, in1=hh, op=ALU.mult),
                     reads=[P.sb(O_GGF + h * TP, TP), buf_v(O_HH, h, TP), buf_v(O_IB, h, TP)], writes=[buf_v(O_A2M, h, TP)])
            for (seqs, col0, n, count) in groups:
                for si, s in enumerate(seqs):
                    cl = col0 + si * n + n - 1
                    src = S[:, O_HH:O_HH + 8 * TP].rearrange("p (j c) -> p j c", c=TP)[:, :, cl:cl + 1]
                    sb0 = ST_OFF + SH_OFF + o_ * 8 * NSEQ
                    dst = S[:, sb0:sb0 + 8 * NSEQ].rearrange("p (j s) -> p j s", s=NSEQ)[:, :, s:s + 1]
                    P.op("dve", lambda e, dst=dst, src=src: e.tensor_copy(out=dst, in_=src),
                         reads=[P.sb(O_HH, 8 * TP)], writes=[st_v])
            out_proj_and_ln1(l, T, O_A2M, O_RA, ("b_out_o", o_))

        colbases = []
        cb_ = 0
        for c in range(nchunks):
            colbases.append(cb_)
            cb_ += chunk_groups(c)[1]

        def load_x(c):
            _, Tc = chunk_groups(c)
            xo_ = X_OFF if c % 2 == 0 else X1_OFF
            src = xin[:, colbases[c]:colbases[c] + Tc].rearrange("(k p) t -> p k t", p=128)
            dst = rap(RNAME[xo_], xo_, xo_ + 8 * TP).rearrange("p (k t) -> p k t", t=TP)[:, :, 0:Tc]
            P.dma("pool", "x" if c % 2 == 0 else "x1", lambda e, dst=dst, src=src: e.dma_start(out=dst, in_=src), writes=[P.sb(xo_, 8 * TP)])

        load_x(0)
        pump(R_SLOTS)
        P.dma("pool", "p2", lambda e: e.dma_start(out=rap("GW", GW_OFF, GW_OFF + 4096), in_=gws), writes=[gw_v])
        for c in range(nchunks):
            groups, T = chunk_groups(c)
            colbase = colbases[c]
            curx["off"] = X_OFF if c % 2 == 0 else X1_OFF
            if c + 1 < nchunks:
                load_x(c + 1)
            for l in range(DEPTH):
                if l % 2 == 0:
                    even_layer(l, T, groups)
                else:
                    odd_layer(l, T, groups)
                mlp_and_ln2(l, T, last=(l == DEPTH - 1))
            ysrc = S[:, YO_OFF:YO_OFF + 8 * TP].rearrange("p (k t) -> p k t", t=TP)[:, :, 0:T]
            ydst = yout[:, colbase:colbase + T].rearrange("(k p) t -> p k t", p=128)
            P.dma("sp", "y", lambda e, ydst=ydst, ysrc=ysrc: e.dma_start(out=ydst, in_=ysrc), reads=[P.sb(YO_OFF, 8 * TP)])
        assert wcur["g"] == wstate["total"] == wstate["issued"], (wcur, wstate)
        P.dma("sp", "so", lambda e: e.dma_start(out=sto, in_=S[:, ST_OFF:ST_OFF + NST]), reads=[st_v])
        P.wait_all("sp", "y")
        P.wait_all("sp", "so")

        @block.sync
        def _(eng):
            P.replay("sp", eng)

        @block.gpsimd
        def _(eng):
            P.replay("pool", eng)

        @block.tensor
        def _(eng):
            P.replay("pe", eng)

        @block.scalar
        def _(eng):
            P.replay("act", eng)

        @block.vector
        def _(eng):
            P.replay("dve", eng)

    return nc


def _fm(v, nch):
    return np.ascontiguousarray(np.asarray(v, np.float32).reshape(nch, 128).T)


def _prep_shared(inp):
    f32 = np.float32
    wsrc = {k: np.asarray(inp[k], f32) for k in ("w_in_e", "w_out_e", "w_in_o", "w_out_o", "w_mlp1", "w_mlp2")}
    wt = np.empty((NTILES, 128, 4096), f32)
    for t, (name, idx, g) in enumerate(PLAN):
        W = wsrc[name][idx] if name in wsrc else None
        if name == "diag_a":
            w = np.asarray(inp["conv_a_w"], f32)[idx]
            tl = np.zeros((128, 4096), f32)
            pp = np.arange(128)
            ntp = 31 - K0A
            for mm in range(g * 32, min(g * 32 + 32, 4 * ntp)):
                j, k = divmod(mm, ntp)
                tl[pp, (mm % 32) * 128 + pp] = w[K0A + k, j * 128:(j + 1) * 128]
            wt[t] = tl
            continue
        if name == "diag_c":
            w = np.asarray(inp["conv_c_w"], f32)[idx]
            tl = np.zeros((128, 4096), f32)
            pp = np.arange(128)
            for mm in range(32):
                j, k = divmod(mm, 4)
                tl[pp, mm * 128 + pp] = w[k, j * 128:(j + 1) * 128]
            wt[t] = tl
            continue
        if name == "w_mlp2":
            blk = W[:, g * 128:(g + 1) * 128]
            wt[t] = blk.reshape(32, 128, 128).transpose(1, 0, 2).reshape(128, 4096)
        else:
            blk = W[:, g * 512:(g + 1) * 512]
            wt[t] = blk.reshape(8, 128, 512).transpose(1, 0, 2).reshape(128, 4096)
    gw = np.empty((128, 2, 2, 8, 128), f32)
    for o in range(2):
        gw[:, o, 0] = np.asarray(inp["w_gate_a"], f32)[o].transpose(1, 0, 2)
        gw[:, o, 1] = np.asarray(inp["w_gate_x"], f32)[o].transpose(1, 0, 2)
    gw = gw.reshape(128, 4096)
    prm = np.zeros((128, NPRM), f32)

    def put(name, arr):
        prm[:, PCOL[name]:PCOL[name] + arr.shape[1]] = arr

    for l in range(DEPTH):
        for nm in ("ln1_g", "ln1_b", "ln2_g", "ln2_b"):
            put((nm, l), _fm(inp[nm][l], 8))
    for e in range(2):
        put(("b_in_e", e), _fm(inp["b_in_e"][e], 20))
        w = np.asarray(inp["conv_a_w"], f32)[e]
        put(("conv_a_w", e), w.reshape(31, 4, 128).transpose(2, 1, 0).reshape(128, 124))
        put(("conv_a_b", e), _fm(inp["conv_a_b"][e], 4))
        put(("ln_a_g", e), _fm(inp["ln_a_g"][e], 4))
        put(("ln_a_b", e), _fm(inp["ln_a_b"][e], 4))
        w = np.asarray(inp["conv_b_w"], f32)[e]
        put(("conv_b_w", e), w.reshape(3, 4, 128).transpose(2, 1, 0).reshape(128, 12))
        put(("b_out_e", e), _fm(inp["b_out_e"][e], 8))
    for o in range(2):
        put(("b_in_o", o), _fm(inp["b_in_o"][o], 16))
        w = np.asarray(inp["conv_c_w"], f32)[o]
        put(("conv_c_w", o), w.reshape(4, 8, 128).transpose(2, 1, 0).reshape(128, 32))
        put(("conv_c_b", o), _fm(inp["conv_c_b"][o], 8))
        put(("b_gate_a", o), _fm(inp["b_gate_a"][o], 8))
        put(("b_gate_x", o), _fm(inp["b_gate_x"][o], 8))
        put(("lru_lambda", o), _fm(inp["lru_lambda"][o], 8))
        put(("b_out_o", o), _fm(inp["b_out_o"][o], 8))
    return wt, gw, prm


def _prep_core(inp, b):
    f32 = np.float32
    xin = np.empty((D, NTOK), f32)
    xin[:, :NMETA] = np.asarray(inp["meta_tokens"], f32).T
    xin[:, NMETA:NPROMPT] = np.asarray(inp["x_prompt"][b], f32).T
    xs = np.asarray(inp["x_sample"][NSAMP * b:NSAMP * (b + 1)], f32)
    xin[:, NPROMPT:] = xs.reshape(NSAMP * TS, D).T
    st = np.zeros((128, NST), f32)
    sl = slice(NSAMP * b, NSAMP * (b + 1))
    sa = np.asarray(inp["state_conv_a"], f32)[:, sl]
    v = st[:, SA_OFF:SB_OFF].reshape(128, 2, 4, NSEQ, 30)
    v[:, :, :, 1:, :] = sa.reshape(2, NSAMP, 30, 4, 128).transpose(4, 0, 3, 1, 2)
    sb = np.asarray(inp["state_conv_b"], f32)[:, sl]
    v = st[:, SB_OFF:SC_OFF].reshape(128, 2, 4, NSEQ, 2)
    v[:, :, :, 1:, :] = sb.reshape(2, NSAMP, 2, 4, 128).transpose(4, 0, 3, 1, 2)
    sc = np.asarray(inp["state_conv_c"], f32)[:, sl]
    v = st[:, SC_OFF:SH_OFF].reshape(128, 2, 8, NSEQ, 3)
    v[:, :, :, 1:, :] = sc.reshape(2, NSAMP, 3, 8, 128).transpose(4, 0, 3, 1, 2)
    sh = np.asarray(inp["state_lru"], f32)[:, sl]
    v = st[:, SH_OFF:NST].reshape(128, 2, 8, NSEQ)
    v[:, :, :, 1:] = sh.reshape(2, NSAMP, 8, 128).transpose(3, 0, 2, 1)
    return xin, st


_NC_CACHE = {}


def kernel(**inp):
    B = 8
    wt, gw, prm = _prep_shared(inp)
    in_maps = []
    for b in range(B):
        xin, st = _prep_core(inp, b)
        in_maps.append({"xin": xin, "wts": wt, "gws": gw, "prm": prm, "sti": st})
    if "nc" not in _NC_CACHE:
        _NC_CACHE["nc"] = build_program()
    nc = _NC_CACHE["nc"]
    res = run_bass_kernel_spmd(nc, in_maps, core_ids=list(range(B)))
    f32 = np.float32
    y_prompt = np.empty((B, SEQ, D), f32)
    y_sample = np.empty((B * NSAMP, TS, D), f32)
    sa_p = np.empty((2, B, 30, 512), f32)
    sb_p = np.empty((2, B, 2, 512), f32)
    sc_p = np.empty((2, B, 3, 1024), f32)
    sh_p = np.empty((2, B, 1024), f32)
    sa_s = np.empty((2, B * NSAMP, 30, 512), f32)
    sb_s = np.empty((2, B * NSAMP, 2, 512), f32)
    sc_s = np.empty((2, B * NSAMP, 3, 1024), f32)
    sh_s = np.empty((2, B * NSAMP, 1024), f32)
    for b in range(B):
        r = res.results[b]
        yo = np.asarray(r["yout"], f32)
        y_prompt[b] = yo[:, NMETA:NPROMPT].T
        y_sample[NSAMP * b:NSAMP * (b + 1)] = yo[:, NPROMPT:].T.reshape(NSAMP, TS, D)
        st = np.asarray(r["sto"], f32)
        sl = slice(NSAMP * b, NSAMP * (b + 1))
        v = st[:, SA_OFF:SB_OFF].reshape(128, 2, 4, NSEQ, 30)
        t = v.transpose(1, 3, 4, 2, 0).reshape(2, NSEQ, 30, 512)
        sa_p[:, b] = t[:, 0]
        sa_s[:, sl] = t[:, 1:]
        v = st[:, SB_OFF:SC_OFF].reshape(128, 2, 4, NSEQ, 2)
        t = v.transpose(1, 3, 4, 2, 0).reshape(2, NSEQ, 2, 512)
        sb_p[:, b] = t[:, 0]
        sb_s[:, sl] = t[:, 1:]
        v = st[:, SC_OFF:SH_OFF].reshape(128, 2, 8, NSEQ, 3)
        t = v.transpose(1, 3, 4, 2, 0).reshape(2, NSEQ, 3, 1024)
        sc_p[:, b] = t[:, 0]
        sc_s[:, sl] = t[:, 1:]
        v = st[:, SH_OFF:NST].reshape(128, 2, 8, NSEQ)
        t = v.transpose(1, 3, 2, 0).reshape(2, NSEQ, 1024)
        sh_p[:, b] = t[:, 0]
        sh_s[:, sl] = t[:, 1:]
    return (y_prompt, y_sample, sa_p, sb_p, sc_p, sh_p, sa_s, sb_s, sc_s, sh_s)
```
